# Optimizing a Trainium2 kernel written in Bass

```python
import math
import jax, jax.numpy as jnp
from jax import lax
import numpy as np

D_MODEL = 1024
BATCH = 16
SEQ = 4096
DEPTH = 1

EXPAND = 2
D_INNER = EXPAND * D_MODEL
D_SB = D_INNER // 2
D_SSD = D_INNER - D_SB
SB_HEAD_DIM = 64
SB_HEADS = D_SB // SB_HEAD_DIM
SB_BLOCK = 128
SSD_HEAD_DIM = 64
SSD_HEADS = D_SSD // SSD_HEAD_DIM
SSD_GROUPS = 2
SSD_STATE = 128
SSD_CHUNK = 128
CONV_W = 4
CONV_DIM = D_SSD + 2 * SSD_GROUPS * SSD_STATE
D_IN_PROJ = 4 * D_SB + D_SSD + CONV_DIM + SSD_HEADS
EPS = 1e-6

kernel_name = "hymba_stickbreaking_ssd_layer"


def rmsnorm(x, w):
    xf = x.astype(jnp.float32)
    y = xf * lax.rsqrt(jnp.mean(xf * xf, axis=-1, keepdims=True) + EPS)
    return (y * w.astype(jnp.float32)).astype(x.dtype)


def causal_depthwise_conv(u, w, b):
    l = u.shape[1]
    up = jnp.pad(u, ((0, 0), (CONV_W - 1, 0), (0, 0)))
    out = b
    for i in range(CONV_W):
        out = out + up[:, i:i + l] * w[i]
    return out


def stick_breaking_attention(q, k, v):
    b, l, h, d = q.shape
    nb = l // SB_BLOCK
    scale = 1.0 / math.sqrt(d)
    qb = q.astype(jnp.float32).reshape(b, nb, SB_BLOCK, h, d).transpose(1, 0, 2, 3, 4)
    kf = k.astype(jnp.float32)
    vf = v.astype(jnp.float32)
    key_pos = jnp.arange(l)

    def block(args):
        qi, i = args
        z = jnp.einsum('bqhd,bkhd->bhqk', qi, kf) * scale
        q_pos = i * SB_BLOCK + jnp.arange(SB_BLOCK)
        mask = key_pos[None, :] < q_pos[:, None]
        log_keep = jnp.where(mask, jax.nn.log_sigmoid(-z), 0.0)
        rest = lax.cumsum(log_keep, axis=3, reverse=True) - log_keep
        w = jnp.where(mask, jnp.exp(jax.nn.log_sigmoid(z) + rest), 0.0)
        return jnp.einsum('bhqk,bkhd->bqhd', w, vf)

    o = lax.map(block, (qb, jnp.arange(nb)))
    return o.transpose(1, 0, 2, 3, 4).reshape(b, l, h, d)


def ssd_chunked(xh, dt, a, Bm, Cm):
    b, l, h, p = xh.shape
    g, n = Bm.shape[2], Bm.shape[3]
    hpg = h // g
    c = l // SSD_CHUNK
    Q = SSD_CHUNK
    x = (xh.astype(jnp.float32) * dt[..., None]).reshape(b, c, Q, g, hpg, p)
    dA = (dt * a).reshape(b, c, Q, g, hpg).transpose(0, 1, 3, 4, 2)
    Bc = Bm.astype(jnp.float32).reshape(b, c, Q, g, n)
    Cc = Cm.astype(jnp.float32).reshape(b, c, Q, g, n)
    a_cs = jnp.cumsum(dA, axis=-1)

    seg = a_cs[..., :, None] - a_cs[..., None, :]
    tri = jnp.tril(jnp.ones((Q, Q), dtype=bool))
    Lmat = jnp.exp(jnp.where(tri, seg, -jnp.inf))
    cb = jnp.einsum('bctgn,bcsgn->bcgts', Cc, Bc)
    wts = cb[:, :, :, None] * Lmat
    y_diag = jnp.einsum('bcgkts,bcsgkp->bctgkp', wts, x)

    decay_to_end = jnp.exp(a_cs[..., -1:] - a_cs)
    states = jnp.einsum('bcsgn,bcgks,bcsgkp->bcgkpn', Bc, decay_to_end, x)
    chunk_decay = jnp.exp(a_cs[..., -1])

    def step(carry, inp):
        st, dec = inp
        return carry * dec[..., None, None] + st, carry

    init = jnp.zeros((b, g, hpg, p, n), jnp.float32)
    _, states_in = lax.scan(step, init, (jnp.moveaxis(states, 1, 0), jnp.moveaxis(chunk_decay, 1, 0)))
    states_in = jnp.moveaxis(states_in, 0, 1)

    y_off = jnp.einsum('bctgn,bcgkpn,bcgkt->bctgkp', Cc, states_in, jnp.exp(a_cs))
    return (y_diag + y_off).reshape(b, l, h, p)


def hybrid_layer(x, norm_w, w_in, q_norm_w, k_norm_w, conv_w, conv_b, dt_bias,
                 A_log, D_skip, sb_norm_w, ssd_norm_w, w_out):
    b, l, _ = x.shape
    hn = rmsnorm(x, norm_w)
    proj = hn @ w_in
    i1 = D_SB; i2 = 2 * D_SB; i3 = 3 * D_SB; i4 = 4 * D_SB
    i5 = i4 + D_SSD; i6 = i5 + CONV_DIM
    q, k, v, z_sb, z_ssd, xBC, dt_raw = jnp.split(proj, [i1, i2, i3, i4, i5, i6], axis=-1)

    q = rmsnorm(q.reshape(b, l, SB_HEADS, SB_HEAD_DIM), q_norm_w)
    k = rmsnorm(k.reshape(b, l, SB_HEADS, SB_HEAD_DIM), k_norm_w)
    v = v.reshape(b, l, SB_HEADS, SB_HEAD_DIM)
    o_sb = stick_breaking_attention(q, k, v).reshape(b, l, D_SB).astype(x.dtype)
    y_sb = rmsnorm(o_sb * jax.nn.silu(z_sb), sb_norm_w)

    xBC = jax.nn.silu(causal_depthwise_conv(xBC, conv_w, conv_b))
    xs, Bm, Cm = jnp.split(xBC, [D_SSD, D_SSD + SSD_GROUPS * SSD_STATE], axis=-1)
    xs = xs.reshape(b, l, SSD_HEADS, SSD_HEAD_DIM)
    dt = jax.nn.softplus(dt_raw.astype(jnp.float32) + dt_bias.astype(jnp.float32))
    a = -jnp.exp(A_log.astype(jnp.float32))
    y = ssd_chunked(xs, dt, a,
                    Bm.reshape(b, l, SSD_GROUPS, SSD_STATE),
                    Cm.reshape(b, l, SSD_GROUPS, SSD_STATE))
    y = y + D_skip.astype(jnp.float32)[:, None] * xs.astype(jnp.float32)
    y = y.reshape(b, l, D_SSD).astype(x.dtype)
    y_ssd = rmsnorm(y * jax.nn.silu(z_ssd), ssd_norm_w)

    mixed = jnp.concatenate([y_sb, y_ssd], axis=-1)
    return x + mixed @ w_out


def setup_inputs(seed: int = 0) -> dict:
    key = jax.random.key(seed)
    ks = jax.random.split(key, 14)
    f32 = jnp.float32
    x = jax.random.normal(ks[0], (BATCH, SEQ, D_MODEL), f32)
    norm_w = 1.0 + 0.02 * jax.random.normal(ks[1], (DEPTH, D_MODEL), f32)
    w_in = jax.random.normal(ks[2], (DEPTH, D_MODEL, D_IN_PROJ), f32) * D_MODEL ** -0.5
    q_norm_w = 1.0 + 0.02 * jax.random.normal(ks[3], (DEPTH, SB_HEAD_DIM), f32)
    k_norm_w = 1.0 + 0.02 * jax.random.normal(ks[4], (DEPTH, SB_HEAD_DIM), f32)
    conv_w = jax.random.normal(ks[5], (DEPTH, CONV_W, CONV_DIM), f32) * CONV_W ** -0.5
    conv_b = 0.01 * jax.random.normal(ks[6], (DEPTH, CONV_DIM), f32)
    u = jax.random.uniform(ks[7], (DEPTH, SSD_HEADS), f32)
    dt0 = jnp.exp(u * (math.log(0.1) - math.log(0.001)) + math.log(0.001))
    dt_bias = dt0 + jnp.log(-jnp.expm1(-dt0))
    A_log = jnp.log(jax.random.uniform(ks[8], (DEPTH, SSD_HEADS), f32, minval=1.0, maxval=16.0))
    D_skip = 1.0 + 0.02 * jax.random.normal(ks[9], (DEPTH, SSD_HEADS), f32)
    sb_norm_w = 1.0 + 0.02 * jax.random.normal(ks[10], (DEPTH, D_SB), f32)
    ssd_norm_w = 1.0 + 0.02 * jax.random.normal(ks[11], (DEPTH, D_SSD), f32)
    w_out = jax.random.normal(ks[12], (DEPTH, D_INNER, D_MODEL), f32) * D_INNER ** -0.5
    return {"x": x, "norm_w": norm_w, "w_in": w_in, "q_norm_w": q_norm_w,
            "k_norm_w": k_norm_w, "conv_w": conv_w, "conv_b": conv_b,
            "dt_bias": dt_bias, "A_log": A_log, "D_skip": D_skip,
            "sb_norm_w": sb_norm_w, "ssd_norm_w": ssd_norm_w, "w_out": w_out}


def reference(x, norm_w, w_in, q_norm_w, k_norm_w, conv_w, conv_b, dt_bias,
              A_log, D_skip, sb_norm_w, ssd_norm_w, w_out):
    for layer in range(DEPTH):
        x = hybrid_layer(x, norm_w[layer], w_in[layer], q_norm_w[layer], k_norm_w[layer],
                         conv_w[layer], conv_b[layer], dt_bias[layer], A_log[layer],
                         D_skip[layer], sb_norm_w[layer], ssd_norm_w[layer], w_out[layer])
    return x
```

```python
import numpy as np
import ml_dtypes
from contextlib import ExitStack
import concourse.bass as bass
import concourse.mybir as mybir
from concourse.bass_utils import run_bass_kernel_spmd

F32 = mybir.dt.float32
BF16 = mybir.dt.bfloat16
F32R = mybir.dt.float32r
AF = mybir.ActivationFunctionType
ALU = mybir.AluOpType
EPS = 1e-6
NDUMMY = 4
DIN = 6672
NCORES = 8


class Buf:
    __slots__ = ("name", "w", "r", "excl")

    def __init__(self, name, excl=False):
        self.name = name
        self.w = None
        self.r = []
        self.excl = excl


class _Eng:
    def __init__(self, sched, name, handle, is_pe=False):
        self.name = name
        self.h = handle
        self.is_pe = is_pe
        self.seg = 0
        self.cnt = 0
        self.sem = sched.new_sem(f"c_{name}_0")
        self.known = {}


class _DmaSem:
    def __init__(self, sched, name):
        self.name = name
        self.gen = 0
        self.cnt = 0
        self.sem = sched.new_sem(f"d_{name}_0")


class Sched:
    SEG = 12000

    def __init__(self, nc, stack):
        self.nc = nc
        self.stack = stack
        self.nsem = 0
        self.pe = _Eng(self, "pe", nc.tensor, is_pe=True)
        self.act = _Eng(self, "act", nc.scalar)
        self.dve = _Eng(self, "dve", nc.vector)
        self.pool = _Eng(self, "pool", nc.gpsimd)
        self.sp = _Eng(self, "sp", nc.sync)
        self.engs = [self.pe, self.act, self.dve, self.pool, self.sp]
        self.dsems = []
        self.nwaits = 0
        self.ninst = 0

    def new_sem(self, name):
        self.nsem += 1
        return self.stack.enter_context(self.nc.semaphore(name))

    def dma_sem(self, name):
        d = _DmaSem(self, name)
        self.dsems.append(d)
        return d

    def _wait(self, eng, tok):
        kind, obj, sem, val = tok
        if kind == "e" and obj is eng and (eng.is_pe or eng is self.sp):
            return
        key = id(sem)
        if eng.known.get(key, 0) >= val:
            return
        eng.known[key] = val
        self.nwaits += 1
        eng.h.wait_ge(sem, val)

    @staticmethod
    def _deps(reads, writes):
        deps = []
        for b in reads:
            if b.w is not None:
                deps.append(b.w)
        for b in writes:
            if b.w is not None:
                deps.append(b.w)
            deps.extend(b.r)
        return deps

    @staticmethod
    def _mark(tok, reads, writes):
        for b in reads:
            b.r.append(tok)
            if len(b.r) > 64:
                b.r = b.r[-48:]
        for b in writes:
            b.w = tok
            b.r = []

    def op(self, eng, fn, reads=(), writes=(), relax=False):
        ex = [b for b in reads if b.excl]
        if ex:
            writes = list(writes) + [b for b in ex if b not in writes]
            reads = [b for b in reads if not b.excl]
        for d in self._deps(reads, writes):
            if relax and d[0] == "e" and d[1] is eng:
                continue
            self._wait(eng, d)
        if eng.cnt >= self.SEG:
            eng.seg += 1
            eng.cnt = 0
            eng.sem = self.new_sem(f"c_{eng.name}_{eng.seg}")
        eng.cnt += 1
        fn(eng.h).then_inc(eng.sem, 1)
        self.ninst += 1
        tok = ("e", eng, eng.sem, eng.cnt)
        self._mark(tok, reads, writes)
        return tok

    def dma(self, dsem, out, in_, reads=(), writes=(), **kw):
        eng = self.sp
        for d in self._deps(reads, writes):
            self._wait(eng, d)
        if dsem.cnt + 16 > self.SEG:
            dsem.gen += 1
            dsem.cnt = 0
            dsem.sem = self.new_sem(f"d_{dsem.name}_{dsem.gen}")
        dsem.cnt += 16
        eng.h.dma_start(out=out, in_=in_, **kw).then_inc(dsem.sem, 16)
        tok = ("d", dsem, dsem.sem, dsem.cnt)
        self._mark(tok, reads, writes)
        return tok

    def barrier(self):
        toks = []
        for e in self.engs:
            if e.cnt > 0:
                toks.append(("e", e, e.sem, e.cnt))
        for d in self.dsems:
            if d.cnt > 0:
                toks.append(("d", d, d.sem, d.cnt))
        for e in self.engs:
            for t in toks:
                if t[1] is not e:
                    self._wait(e, t)


def _consts():
    p = np.arange(128)[:, None]
    f = np.arange(128)[None, :]
    ident = (p == f).astype(np.float32)
    ntri = -(p >= f).astype(np.float32)
    nundo = -(p < f).astype(np.float32)
    c = np.arange(512)[None, :]
    neg = np.zeros((128, 4, 512), np.float32)
    for r in range(4):
        neg[:, r, :] = np.where(c <= 128 * r + p, -30000.0, 0.0)
    bd64b = ((p // 64) == (f // 64)).astype(np.float32) / 64.0
    cb = np.concatenate([ident, ntri, nundo, bd64b, neg.reshape(128, 2048)], axis=1).astype(ml_dtypes.bfloat16)
    tri_le = (p <= f).astype(np.float32)
    lt_strict = (p > f).astype(np.float32)
    ones = np.ones((128, 128), np.float32)
    bd64 = ((p // 64) == (f // 64)).astype(np.float32) / 64.0
    cf = np.concatenate([tri_le, lt_strict, ones, bd64], axis=1).astype(np.float32)
    return cf, cb


NPRM = 8 + 2 + 48 + 12 + 48 + 8 + 8


def _params(norm_w, q_norm_w, k_norm_w, conv_w, conv_b, dt_bias, A_log, D_skip, sb_norm_w, ssd_norm_w):
    prm = np.zeros((128, NPRM), np.float32)
    o = 0
    prm[:, o:o + 8] = norm_w.reshape(8, 128).T; o += 8
    prm[:, o] = np.tile(q_norm_w.reshape(64), 2); o += 1
    prm[:, o] = np.tile(k_norm_w.reshape(64), 2); o += 1
    cw = conv_w.reshape(4, 12, 128)
    prm[:, o:o + 48] = cw.transpose(2, 1, 0).reshape(128, 48); o += 48
    prm[:, o:o + 12] = conv_b.reshape(12, 128).T; o += 12
    prm[:, o:o + 16] = np.broadcast_to(dt_bias.reshape(1, 16), (128, 16)); o += 16
    prm[:, o:o + 16] = np.broadcast_to(A_log.reshape(1, 16), (128, 16)); o += 16
    prm[:, o:o + 16] = np.broadcast_to(D_skip.reshape(1, 16), (128, 16)); o += 16
    prm[:, o:o + 8] = sb_norm_w.reshape(8, 128).T; o += 8
    prm[:, o:o + 8] = ssd_norm_w.reshape(8, 128).T; o += 8
    assert o == NPRM
    return prm


class _Stop(Exception):
    pass


_STOPPED = [False]


class _Phase(ExitStack):
    def __exit__(self, et, ev, tb):
        super().__exit__(None, None, None)
        if et is _Stop:
            _STOPPED[0] = True
            return True
        return False


def build(L, NSEQ, stop=None):
    NB = L // 128
    NT = L // 512
    nc = bass.Bass("TRN2", target_bir_lowering=False)
    x_d = nc.dram_tensor("x", [NSEQ, L, 1024], F32, kind="ExternalInput").ap()
    win_d = nc.dram_tensor("w_in", [1024, DIN], F32, kind="ExternalInput").ap()
    wout_d = nc.dram_tensor("w_out", [2048, 1024], F32, kind="ExternalInput").ap()
    prm_d = nc.dram_tensor("prm", [128, NPRM], F32, kind="ExternalInput").ap()
    cf_d = nc.dram_tensor("cf", [128, 512], F32, kind="ExternalInput").ap()
    cb_d = nc.dram_tensor("cb", [128, 512 + 2048], BF16, kind="ExternalInput").ap()
    out_d = nc.dram_tensor("out", [NSEQ, L, 1024], F32, kind="ExternalOutput").ap()
    gscr = nc.dram_tensor("gscr", [1024, L], BF16, kind="Internal").ap()

    with ExitStack() as top:
      S = Sched(nc, top)
      _STOPPED[0] = False
      try:
          PE, ACT, DVE, POOL = S.pe, S.act, S.dve, S.pool

          ucnt = [0]

          def sbt(stk, name, shape, dt):
              ucnt[0] += 1
              return stk.enter_context(nc.sbuf_tensor(f"s{ucnt[0]}_{name}", shape, dt))

          PS = [top.enter_context(nc.psum_tensor(f"ps{i}", [128, 1024], F32)) for i in range(4)]
          bPS = [Buf(f"bank{i}", excl=True) for i in range(8)]

          def bank(i):
              return PS[i // 2][:, (i % 2) * 512:(i % 2) * 512 + 512]

          def bank_bf(i):
              return bank(i).bitcast(BF16)

          def bank2(i):
              return PS[i // 2][:, :]

          prm = sbt(top, "prm", [128, NPRM], F32)
          cf = sbt(top, "cf", [128, 512], F32)
          cb = sbt(top, "cbf", [128, 512 + 2048], BF16)
          qw8 = sbt(top, "qw8", [128, 1], F32)
          mhalf = sbt(top, "mhalf", [128, 1], F32)
          aneg = sbt(top, "aneg", [128, 16], F32)
          RSB = sbt(top, "rsb", [128, NB], F32)
          SSQ = sbt(top, "ssq", [128, NB], F32)
          xt = [sbt(top, f"xt{i}", [128, 1024], F32) for i in range(2)]
          xs2 = [sbt(top, f"xs_bf{i}", [128, 1024], BF16) for i in range(2)]; bxs2 = [Buf("xs0"), Buf("xs1")]
          nst2 = [sbt(top, f"nst{i}", [128, 4], F32) for i in range(2)]; bnst2 = [Buf("nst0"), Buf("nst1")]
          wst = [sbt(top, f"wst{i}", [128, 1024], F32) for i in range(2)]
          bconst = Buf("const"); bxt = [Buf("xt0"), Buf("xt1")]; bxs = Buf("xs"); bjunk = Buf("junk")
          bnst = Buf("nst"); bwst = [Buf(f"wst{i}") for i in range(2)]
          bRSB = Buf("rsb"); bSSQ = Buf("ssq"); bgscr = Buf("gscr"); bgscr2 = [Buf("gscr0"), Buf("gscr1")]

          ld = S.dma_sem("ld")
          ldx = [S.dma_sem("ldx0"), S.dma_sem("ldx1")]
          ldw = [S.dma_sem("ldw0"), S.dma_sem("ldw1")]
          stg = [S.dma_sem("stg0"), S.dma_sem("stg1")]
          ldg = S.dma_sem("ldg")
          sto = [S.dma_sem("sto0"), S.dma_sem("sto1")]

          S.dma(ld, prm[:, :], prm_d, writes=[bconst])
          S.dma(ld, cf[:, :], cf_d, writes=[bconst])
          S.dma(ld, cb[:, :], cb_d, writes=[bconst])
          if stop == 0:
              raise _Stop()

          o = 0
          NORMW = prm[:, o:o + 8]; o += 8
          QW = prm[:, o:o + 1]; o += 1
          KW = prm[:, o:o + 1]; o += 1
          CONVW = prm[:, o:o + 48]; o += 48
          CONVB = prm[:, o:o + 12]; o += 12
          DTB = prm[:, o:o + 16]; o += 16
          ALOG = prm[:, o:o + 16]; o += 16
          DSK = prm[:, o:o + 16]; o += 16
          SBW = prm[:, o:o + 8]; o += 8
          SSDW = prm[:, o:o + 8]; o += 8
          TRI_LE = cf[:, 0:128]; LT_STRICT = cf[:, 128:256]; ONES = cf[:, 256:384]; BD64 = cf[:, 384:512]
          IDENT = cb[:, 0:128]; NTRI = cb[:, 128:256]; NUNDO = cb[:, 256:384]; BD64B = cb[:, 384:512]

          def NEG(r):
              return cb[:, 512 + r * 512:512 + (r + 1) * 512]

          S.op(DVE, lambda h: h.tensor_scalar(out=qw8[:, :], in0=QW, scalar1=0.125, scalar2=None, op0=ALU.mult),
               reads=[bconst], writes=[bconst])
          S.op(POOL, lambda h: h.memset(mhalf[:, :], -0.5), reads=[bconst], writes=[bconst])
          S.op(ACT, lambda h: h.activation(out=aneg[:, :], in_=ALOG, func=AF.Exp), reads=[bconst], writes=[bconst])
          S.op(DVE, lambda h: h.tensor_scalar(out=aneg[:, :], in0=aneg[:, :], scalar1=-1.0, scalar2=None, op0=ALU.mult),
               reads=[bconst], writes=[bconst])

          xcnt = [0]

          def norm_blocks(seq, blks, dst_fn, bdst):
              slots = []

              def stats(i):
                  blk = blks[i]
                  s = xcnt[0] % 2
                  xcnt[0] += 1
                  slots.append(s)
                  S.dma(ldx[s], xt[s][:, :], x_d[seq, blk * 128:(blk + 1) * 128, :], writes=[bxt[s]])
                  S.op(ACT, lambda h: h.activation(out=xs2[s][:, :], in_=xt[s][:, :], func=AF.Square, accum_out=nst2[s][:, 0:1]),
                       reads=[bxt[s]], writes=[bxs2[s], bnst2[s]])
                  S.op(ACT, lambda h: h.activation(out=nst2[s][:, 1:2], in_=nst2[s][:, 0:1], func=AF.Ln, scale=1.0 / 1024, bias=EPS),
                       reads=[bnst2[s]], writes=[bnst2[s]])
                  S.op(ACT, lambda h: h.activation(out=nst2[s][:, 2:3], in_=nst2[s][:, 1:2], func=AF.Exp, scale=-0.5),
                       reads=[bnst2[s]], writes=[bnst2[s]])
                  S.op(DVE, lambda h: h.tensor_scalar(out=xs2[s][:, :], in0=xt[s][:, :], scalar1=nst2[s][:, 2:3], scalar2=None,
                                                      op0=ALU.mult), reads=[bxt[s], bnst2[s]], writes=[bxs2[s]])

              def trans(i):
                  s = slots[i]
                  pb = bank_bf(0)
                  for k in range(8):
                      S.op(PE, lambda h, k=k: h.transpose(out=pb[:, k * 128:(k + 1) * 128], in_=xs2[s][:, k * 128:(k + 1) * 128],
                                                          identity=IDENT), reads=[bxs2[s], bconst], writes=[bPS[0]])
                  S.op(DVE, lambda h: h.tensor_tensor(out=dst_fn(i), in0=pb.rearrange("p (k c) -> p k c", k=8),
                                                      in1=NORMW.unsqueeze(2).to_broadcast([128, 8, 128]), op=ALU.mult),
                       reads=[bPS[0], bconst], writes=[bdst])

              stats(0)
              for i in range(len(blks)):
                  if i + 1 < len(blks):
                      stats(i + 1)
                  trans(i)

          wcnt = [0]

          def load_w(src3, dst3, bdst, shape3):
              s = wcnt[0] % 2
              wcnt[0] += 1
              a, b = shape3
              stv = wst[s][:, 0:a * b].rearrange("p (a b) -> p a b", a=a)
              S.dma(ldw[s], stv, src3, writes=[bwst[s]])
              S.op(DVE, lambda h: h.tensor_copy(out=dst3, in_=stv), reads=[bwst[s]], writes=[bdst])

          def silu_from_psum(pin, dst, tmp, btmp, rd, wr, n=512):
              S.op(ACT, lambda h: h.activation(out=tmp, in_=pin, func=AF.Exp, scale=-1.0), reads=rd, writes=[btmp])
              S.op(ACT, lambda h: h.activation(out=tmp, in_=tmp, func=AF.Ln, bias=1.0), reads=[btmp], writes=[btmp])
              S.op(ACT, lambda h: h.activation(out=tmp, in_=tmp, func=AF.Exp, scale=-1.0), reads=[btmp], writes=[btmp])
              S.op(DVE, lambda h: h.tensor_tensor(out=dst, in0=pin, in1=tmp, op=ALU.mult), reads=rd + [btmp], writes=wr)

          for seq in range(NSEQ):
              with _Phase() as ph:
                  if _STOPPED[0]:
                      raise _Stop()
                  hnT = sbt(ph, "hnT", [128, 8, L], BF16); bhnT = Buf("hnT")
                  QPA = sbt(ph, "QPA", [128, L], BF16); QPB = sbt(ph, "QPB", [128, L], BF16)
                  KT2 = [sbt(ph, f"KT{i}", [128, L], BF16) for i in range(2)]
                  VV2 = [sbt(ph, f"VV{i}", [128, NB * 128], BF16) for i in range(2)]
                  ZS = sbt(ph, "ZS", [128, L], BF16)
                  GP2 = [sbt(ph, f"GP{i}", [128, 512], BF16) for i in range(2)]; bGP2 = [Buf("GP0"), Buf("GP1")]
                  bQ = Buf("Q"); bKT = Buf("KT"); bVV = Buf("VV"); bZS = Buf("ZS"); bGP = Buf("GP")
                  bQt = [Buf(f"Q{i}") for i in range(NT)]; bZSt = [Buf(f"ZS{i}") for i in range(NT)]
                  bKT2 = [[Buf(f"KT{a}_{i}") for i in range(NT)] for a in range(2)]
                  bVV2 = [[Buf(f"VV{a}_{i}") for i in range(NT)] for a in range(2)]
                  gcnt = [0]
                  SQ1 = sbt(ph, "SQ1", [128, 512], BF16); bSQ1 = Buf("SQ1")
                  R1s = sbt(ph, "R1s", [128, 512], F32); bR1s = Buf("R1s")
                  ZT = sbt(ph, "ZT", [128, 512], F32); bZT = Buf("ZT")
                  VTS1 = sbt(ph, "VTS1", [128, 512], BF16); bVTS1 = Buf("VTS1")
                  wbf = [[sbt(ph, f"wbf{a}{i}", [128, 8, 128], BF16) for i in range(4)] for a in range(2)]
                  bwbf = [[Buf(f"wbf{a}{i}") for i in range(4)] for a in range(2)]
                  E = [sbt(ph, f"E{i}", [128, 1024], F32) for i in range(3)]
                  SP = [sbt(ph, f"SP{i}", [128, 1024], BF16) for i in range(2)]
                  WR = [sbt(ph, f"WR{i}", [128, 1024], BF16) for i in range(2)]
                  W = [sbt(ph, f"W{i}", [128, 1024], BF16) for i in range(2)]
                  bE = [Buf("E0"), Buf("E1"), Buf("E2")]; bSP = [Buf("SP0"), Buf("SP1")]
                  bWR = [Buf("WR0"), Buf("WR1")]; bW = [Buf("W0"), Buf("W1")]
                  GF = sbt(ph, "GF", [128, 512], F32); bGF = Buf("GF")
                  GSQ = sbt(ph, "GSQ", [128, 512], F32); bGSQ = Buf("GSQ")

                  S.op(POOL, lambda h: h.memset(QPA[:, :], 0.0), writes=[bQ] + bQt)
                  S.op(POOL, lambda h: h.memset(QPB[:, :], 0.0), writes=[bQ] + bQt)
                  S.op(POOL, lambda h: h.memset(SSQ[:, :], 0.0), writes=[bSSQ])

                  def load_pair_w(hp):
                      a = hp % 2
                      for i in range(4):
                          c0 = i * 1024 + hp * 128
                          load_w(win_d[:, c0:c0 + 128].rearrange("(k p) c -> p k c", p=128), wbf[a][i][:, :, :],
                                 bwbf[a][i], (8, 128))

                  load_pair_w(0)
                  norm_blocks(seq, list(range(NB)), lambda i: hnT[:, :, i * 128:(i + 1) * 128], bhnT)
                  if stop == 1:
                      raise _Stop()

                  qcnt = [0]

                  def qknorm(pb, bpb, dsts, wap):
                      s = qcnt[0] % 2
                      qcnt[0] += 1
                      S.op(ACT, lambda h: h.activation(out=SQ[s][:, :], in_=pb, func=AF.Square), reads=[bpb], writes=[bSQ[s]])
                      pm = bank(1)
                      S.op(PE, lambda h: h.matmul(pm, lhsT=BD64, rhs=SQ[s][:, :], start=True, stop=True),
                           reads=[bSQ[s], bconst], writes=[bPS[1]])
                      S.op(ACT, lambda h: h.activation(out=R1[s][:, :], in_=pm, func=AF.Ln, bias=EPS),
                           reads=[bPS[1]], writes=[bR1[s]])
                      S.op(ACT, lambda h: h.activation(out=R1[s][:, :], in_=R1[s][:, :], func=AF.Exp, scale=-0.5),
                           reads=[bR1[s]], writes=[bR1[s]])
                      for (dst, psl, bd) in dsts:
                          S.op(DVE, lambda h, dst=dst, psl=psl: h.scalar_tensor_tensor(
                              out=dst, in0=pb[psl, :], scalar=wap[psl, :], in1=R1[s][psl, :], op0=ALU.mult, op1=ALU.mult),
                              reads=[bpb, bR1[s], bconst], writes=[bd])

                  for hp in range(8):
                      a = hp % 2
                      KT, VV, bKTt, bVVt = KT2[a], VV2[a], bKT2[a], bVV2[a]
                      def proj_units(tt, wset, bwset, KT, VV, bKTt, bVVt):
                          (wq, wk, wv, wz), (bwq, bwk, bwv, bwz) = wset, bwset
                          ts = slice(tt * 512, (tt + 1) * 512)
                          U = []

                          def mm8(wt, bwt):
                              def half(k0):
                                  def f():
                                      for k in range(k0, k0 + 4):
                                          S.op(PE, lambda h, k=k: h.matmul(bank(2), lhsT=wt[:, k, :], rhs=hnT[:, k, ts],
                                                                           start=(k == 0), stop=(k == 7), skip_group_check=True),
                                               reads=[bwt, bhnT], writes=[bPS[2]])
                                  return f
                              return [half(0), half(4)]

                          def sqms():
                              S.op(ACT, lambda h: h.activation(out=SQ1[:, :], in_=bank(2), func=AF.Square), reads=[bPS[2]], writes=[bSQ1])
                              S.op(PE, lambda h: h.matmul(bank(3), lhsT=BD64B, rhs=SQ1[:, :], start=True, stop=True),
                                   reads=[bSQ1, bconst], writes=[bPS[3]])

                          def ln_():
                              S.op(DVE, lambda h: h.tensor_scalar(out=R1s[:, :], in0=bank(3), scalar1=EPS, scalar2=None, op0=ALU.add),
                                   reads=[bPS[3]], writes=[bR1s])

                          def ex_():
                              S.op(POOL, lambda h: h.tensor_tensor(out=R1s[:, :], in0=R1s[:, :], in1=mhalf[:, 0:1].to_broadcast([128, 512]),
                                                                   op=ALU.pow), reads=[bR1s, bconst], writes=[bR1s])

                          def fin(dsts, wap):
                              def f():
                                  for (dst, psl, bd) in dsts:
                                      S.op(DVE, lambda h: h.scalar_tensor_tensor(
                                          out=dst, in0=bank(2)[psl, :], scalar=wap[psl, :], in1=R1s[psl, :], op0=ALU.mult, op1=ALU.mult),
                                          reads=[bPS[2], bR1s, bconst], writes=[bd])
                              return f

                          U += mm8(wq, bwq) + [sqms, ln_, ex_,
                                fin([(QPA[0:64, ts], slice(0, 64), bQt[tt]), (QPB[64:128, ts], slice(64, 128), bQt[tt])], qw8)]
                          U += mm8(wk, bwk) + [sqms, ln_, ex_, fin([(KT[:, ts], slice(0, 128), bKTt[tt])], KW)]
                          U += mm8(wz, bwz)
                          U.append(lambda: S.op(ACT, lambda h: h.activation(out=ZT[:, :], in_=bank(2), func=AF.Exp, scale=-1.0),
                                                reads=[bPS[2]], writes=[bZT]))
                          U.append(lambda: S.op(ACT, lambda h: h.activation(out=ZT[:, :], in_=ZT[:, :], func=AF.Ln, bias=1.0),
                                                reads=[bZT], writes=[bZT]))
                          U.append(lambda: S.op(ACT, lambda h: h.activation(out=ZT[:, :], in_=ZT[:, :], func=AF.Exp, scale=-1.0),
                                                reads=[bZT], writes=[bZT]))
                          U.append(lambda: S.op(DVE, lambda h: h.tensor_tensor(out=ZS[:, ts], in0=bank(2), in1=ZT[:, :], op=ALU.mult),
                                                reads=[bPS[2], bZT], writes=[bZSt[tt]]))
                          U += mm8(wv, bwv)
                          U.append(lambda: S.op(DVE, lambda h: h.tensor_copy(out=VTS1[:, :], in_=bank(2)), reads=[bPS[2]], writes=[bVTS1]))

                          def vtr():
                              pvb = bank_bf(3)
                              for b_ in range(4):
                                  S.op(PE, lambda h, b_=b_: h.transpose(out=pvb[:, b_ * 128:(b_ + 1) * 128],
                                                                        in_=VTS1[:, b_ * 128:(b_ + 1) * 128], identity=IDENT),
                                       reads=[bVTS1, bconst], writes=[bPS[3]])
                          U.append(vtr)
                          U.append(lambda: S.op(DVE, lambda h: h.tensor_copy(out=VV[:, ts], in_=bank_bf(3)[:, 0:512]),
                                                reads=[bPS[3]], writes=[bVVt[tt]]))
                          return U

                      NE = min(3, NT)
                      if hp == 0:
                          for tt in range(NE):
                              for u in proj_units(tt, wbf[a], bwbf[a], KT, VV, bKTt, bVVt):
                                  u()
                      if hp + 1 < 8:
                          load_pair_w(hp + 1)
                      queue = []
                      for tt in range(NE, NT):
                          queue += [(tt, 0, u) for u in proj_units(tt, wbf[a], bwbf[a], KT, VV, bKTt, bVVt)]
                      if hp + 1 < 8:
                          an = 1 - a
                          for tt in range(NE):
                              queue += [(NT + tt, tt + 1, u) for u in proj_units(tt, wbf[an], bwbf[an], KT2[an], VV2[an], bKT2[an], bVV2[an])]
                      rate = len(queue) / float(sum(4 * q_ + 4 for q_ in range(NT)))
                      gstep = [0, 0]
                      if stop == 2:
                          raise _Stop()

                      PZ = [bank2(0), bank2(0)]; bPZ = [[bPS[0], bPS[1]], [bPS[0], bPS[1]]]
                      PX = bank2(4); bPX = [bPS[4], bPS[5]]
                      PO = bank2(6); bPO = [bPS[6], bPS[7]]
                      qp = [QPA, QPB]
                      for qi in range(NT):
                          J = 4 * qi + 4
                          steps = list(range(J - 1, -1, -1))
                          qs = slice(qi * 512, (qi + 1) * 512)
                          while queue and queue[0][0] <= qi:
                              queue.pop(0)[2]()
                              gstep[1] += 1

                          nst_ = len(steps)

                          def c0_(i):
                              j = steps[i]
                              return 128 * (j - 4 * qi) if j >= 4 * qi else 0

                          def v2(ap2, i):
                              return ap2.rearrange("p (h c) -> p h c", h=2)[:, :, c0_(i):512]

                          def zmm(i):
                              j = steps[i]; s = i % 2
                              diag = j >= 4 * qi
                              c0 = c0_(i)
                              for hh in range(2):
                                  o_ap = PZ[s][:, hh * 512 + c0:(hh + 1) * 512]
                                  S.op(PE, lambda h: h.matmul(o_ap, lhsT=KT[:, j * 128:(j + 1) * 128],
                                                              rhs=qp[hh][:, qi * 512 + c0:(qi + 1) * 512],
                                                              start=True, stop=not diag),
                                       reads=[bKTt[j // 4], bQt[qi]], writes=[bPZ[s][hh]])
                                  if diag:
                                      S.op(PE, lambda h: h.matmul(o_ap, lhsT=IDENT, rhs=NEG(j - 4 * qi)[:, c0:512], start=False, stop=True),
                                           reads=[bconst], writes=[bPZ[s][hh]])

                          def expz(i):
                              s = i % 2
                              S.op(ACT, lambda h: h.activation(out=v2(E[i % 3][:, :], i), in_=v2(PZ[s], i), func=AF.Exp),
                                   reads=bPZ[s], writes=[bE[i % 3]])

                          def lnsp(i):
                              s = i % 2
                              S.op(ACT, lambda h: h.activation(out=v2(SP[s][:, :], i), in_=v2(E[i % 3][:, :], i), func=AF.Ln, bias=1.0),
                                   reads=[bE[i % 3]], writes=[bSP[s]])

                          def lmm(i):
                              s = i % 2
                              c0 = c0_(i)
                              for hh in range(2):
                                  S.op(PE, lambda h: h.matmul(PX[:, hh * 512 + c0:(hh + 1) * 512], lhsT=NTRI,
                                                              rhs=SP[s][:, hh * 512 + c0:(hh + 1) * 512], start=(i == 0), stop=False,
                                                              skip_group_check=True),
                                       reads=[bSP[s], bconst], writes=[bPX[hh]])

                          def expx(i):
                              s = i % 2
                              S.op(ACT, lambda h: h.activation(out=v2(WR[s][:, :], i), in_=v2(PX, i), func=AF.Exp), reads=bPX, writes=[bWR[s]])

                          def wmul(i):
                              s = i % 2
                              S.op(DVE, lambda h: h.tensor_tensor(out=v2(W[s][:, :], i), in0=v2(E[i % 3][:, :], i), in1=v2(WR[s][:, :], i),
                                                                  op=ALU.mult),
                                   reads=[bE[i % 3], bWR[s]], writes=[bW[s]])

                          def umm(i):
                              s = i % 2
                              c0 = c0_(i)
                              for hh in range(2):
                                  S.op(PE, lambda h: h.matmul(PX[:, hh * 512 + c0:(hh + 1) * 512], lhsT=NUNDO,
                                                              rhs=SP[s][:, hh * 512 + c0:(hh + 1) * 512], start=False, stop=False,
                                                              skip_group_check=True),
                                       reads=[bSP[s], bconst], writes=[bPX[hh]])

                          def pvmm(i):
                              j = steps[i]; s = i % 2
                              c0 = c0_(i)
                              for hh in range(2):
                                  S.op(PE, lambda h: h.matmul(PO[:, hh * 512 + c0:(hh + 1) * 512], lhsT=VV[:, j * 128:(j + 1) * 128],
                                                              rhs=W[s][:, hh * 512 + c0:(hh + 1) * 512], start=(i == 0), stop=(i == nst_ - 1),
                                                              skip_group_check=True),
                                       reads=[bW[s], bVVt[j // 4]], writes=[bPO[hh]])

                          zmm(0)
                          expz(0)
                          lnsp(0)
                          if nst_ > 1:
                              zmm(1)
                          for i in range(nst_):
                              lmm(i)
                              if i + 1 < nst_:
                                  expz(i + 1)
                              expx(i)
                              if i > 0:
                                  pvmm(i - 1)
                              wmul(i)
                              if i + 1 < nst_:
                                  lnsp(i + 1)
                              if i + 2 < nst_:
                                  zmm(i + 2)
                              if i + 1 < nst_:
                                  umm(i)
                              gstep[0] += 1
                              nu = 0
                              while queue and gstep[1] < gstep[0] * rate and queue[0][1] <= qi:
                                  queue.pop(0)[2]()
                                  gstep[1] += 1
                                  nu += 1
                              if nu == 0 and not queue and i + 1 < nst_:
                                  for _d in range(NDUMMY):
                                      S.op(PE, lambda h: h.matmul(bank(2), lhsT=IDENT, rhs=NEG(_d % 4), start=True, stop=True),
                                           reads=[bconst], writes=[bPS[2]])
                          pvmm(nst_ - 1)
                          for hh in range(2):
                              psl = slice(hh * 64, hh * 64 + 64)
                              S.op(DVE, lambda h: h.tensor_tensor(out=GF[psl, :], in0=PO[psl, hh * 512:(hh + 1) * 512],
                                                                  in1=ZS[psl, qs], op=ALU.mult),
                                   reads=[bPO[hh], bZSt[qi]], writes=[bGF])
                          S.op(POOL, lambda h: h.tensor_tensor(out=GSQ[:, :], in0=GF[:, :], in1=GF[:, :], op=ALU.mult),
                               reads=[bGF], writes=[bGSQ])
                          pss = bank(0)
                          for sub in range(4):
                              S.op(PE, lambda h: h.matmul(pss[:, 2 * sub:2 * sub + 2], lhsT=GSQ[:, sub * 128:(sub + 1) * 128],
                                                          rhs=ONES[:, 0:2], start=True, stop=True),
                                   reads=[bGSQ, bconst], writes=[bPS[0]])
                          S.op(DVE, lambda h: h.tensor_tensor(
                              out=SSQ[:, qi * 4:(qi + 1) * 4], in0=pss[:, 0:8].rearrange("p (a b) -> p a b", b=2)[:, :, 0],
                              in1=SSQ[:, qi * 4:(qi + 1) * 4], op=ALU.add), reads=[bPS[0], bSSQ], writes=[bSSQ])
                          g_ = gcnt[0] % 2
                          gcnt[0] += 1
                          S.op(POOL, lambda h: h.tensor_scalar(out=GP2[g_][:, :], in0=GF[:, :], scalar1=SBW[:, hp:hp + 1], scalar2=None,
                                                               op0=ALU.mult), reads=[bGF, bconst], writes=[bGP2[g_]])
                          S.dma(stg[g_], gscr[hp * 128:(hp + 1) * 128, qs], GP2[g_][:, :], reads=[bGP2[g_]], writes=[bgscr2[g_]])
                      while queue:
                          queue.pop(0)[2]()
                      if stop == 3:
                          raise _Stop()

                  S.op(ACT, lambda h: h.activation(out=RSB[:, :], in_=SSQ[:, :], func=AF.Ln, scale=1.0 / 1024, bias=EPS),
                       reads=[bSSQ], writes=[bRSB])
                  S.op(ACT, lambda h: h.activation(out=RSB[:, :], in_=RSB[:, :], func=AF.Exp, scale=-0.5),
                       reads=[bRSB], writes=[bRSB])
                  S.barrier()

              with _Phase() as ph:
                  if _STOPPED[0]:
                      raise _Stop()
                  wssd = sbt(ph, "wssd", [128, 8, 2576], BF16); bwssd = Buf("wssd")
                  wout = sbt(ph, "wout", [128, 16, 1024], BF16); bwout = Buf("wout")
                  HT = sbt(ph, "HT", [128, 8, 512], BF16); bHT = Buf("HT")
                  SZ = sbt(ph, "SZ", [128, 4, 1024], BF16); bSZ = Buf("SZ")
                  XC = sbt(ph, "XC", [128, 12, 512], BF16); bXC = Buf("XC")
                  CB = [sbt(ph, f"CB{i}", [128, 515], F32) for i in range(2)]; bCB = [Buf("CB0"), Buf("CB1")]
                  CA = [sbt(ph, f"CA{i}", [128, 512], F32) for i in range(2)]; bCA = [Buf("CA0"), Buf("CA1")]
                  TAIL = sbt(ph, "TAIL", [128, 12, 3], F32); bTAIL = Buf("TAIL")
                  DT = sbt(ph, "DT", [128, 64], F32); DA = sbt(ph, "DA", [128, 64], F32); bDT = Buf("DT")
                  GT = sbt(ph, "GT", [128, 16, 512], BF16); bGT = Buf("GT")
                  XS_ = [sbt(ph, f"XS{i}", [128, 1024], BF16) for i in range(2)]
                  XDT_ = [sbt(ph, f"XDT{i}", [128, 1024], BF16) for i in range(2)]
                  XD_ = [sbt(ph, f"XD{i}", [128, 1024], BF16) for i in range(2)]
                  BT_ = [sbt(ph, f"BT{i}", [128, 256], BF16) for i in range(2)]
                  EA_ = [sbt(ph, f"EA{i}", [128, 32], F32) for i in range(2)]
                  bXS_ = [Buf("XS0"), Buf("XS1")]; bXDT_ = [Buf("XDT0"), Buf("XDT1")]; bXD_ = [Buf("XD0"), Buf("XD1")]
                  bBT_ = [Buf("BT0"), Buf("BT1")]; bEA_ = [Buf("EA0"), Buf("EA1")]
                  RR = sbt(ph, "RR", [128, 2048], F32); bRR = Buf("RR")
                  ET = sbt(ph, "ET", [128, 2048], F32); bET = Buf("ET")
                  CBM = sbt(ph, "CBM", [128, 256], F32); bCBM = Buf("CBM")
                  MT_ = [sbt(ph, f"MT{i}", [128, 2048], BF16) for i in range(2)]; bMT_ = [Buf("MT0"), Buf("MT1")]
                  T1 = sbt(ph, "T1", [128, 1024], F32); T2 = sbt(ph, "T2", [128, 1024], BF16); XSD = sbt(ph, "XSD", [128, 1024], BF16); bXSD = Buf("XSD")
                  bT1 = Buf("T1"); bT2 = Buf("T2")
                  GB = sbt(ph, "GB", [128, 1024], BF16); bGB = Buf("GB")
                  ST = sbt(ph, "ST", [128, 1024], F32); STB = sbt(ph, "STB", [128, 1024], BF16)
                  bST = Buf("ST"); bSTB = Buf("STB")
                  SSQ2 = sbt(ph, "SSQ2", [128, 8], F32); bSSQ2 = Buf("SSQ2")

                  for c in range(20):
                      c0 = 4096 + c * 128
                      load_w(win_d[:, c0:c0 + 128].rearrange("(k p) c -> p k c", p=128), wssd[:, :, c * 128:(c + 1) * 128],
                             bwssd, (8, 128))
                  load_w(win_d[:, 6656:6672].rearrange("(k p) c -> p k c", p=128), wssd[:, :, 2560:2576], bwssd, (8, 16))
                  for c in range(16):
                      load_w(wout_d[:, c * 64:(c + 1) * 64].rearrange("(k p) c -> p k c", p=128), wout[:, :, c * 64:(c + 1) * 64],
                             bwout, (16, 64))
                  S.op(POOL, lambda h: h.memset(ST[:, :], 0.0), writes=[bST])
                  S.op(POOL, lambda h: h.memset(STB[:, :], 0.0), writes=[bSTB])
                  bTAILc = [Buf(f"TAIL{c}") for c in range(12)]
                  S.op(POOL, lambda h: h.memset(TAIL[:, :, :], 0.0), writes=[bTAIL] + bTAILc)
                  if stop == 4:
                      raise _Stop()

                  ccnt = [0]
                  for tt in range(NT):
                      norm_blocks(seq, [tt * 4 + bb for bb in range(4)], lambda i: HT[:, :, i * 128:(i + 1) * 128], bHT)
                      for bb in range(4):
                          for n in range(2):
                              pi = 2 + ((bb * 2 + n) % 2)
                              pz = bank(pi)
                              for k in range(8):
                                  S.op(PE, lambda h, k=k: h.matmul(pz, lhsT=HT[:, k, bb * 128:(bb + 1) * 128],
                                                                   rhs=wssd[:, k, n * 512:(n + 1) * 512], start=(k == 0), stop=(k == 7)),
                                       reads=[bHT, bwssd], writes=[bPS[pi]])
                              silu_from_psum(pz, SZ[:, bb, n * 512:(n + 1) * 512], T1[:, n * 512:(n + 1) * 512], bT1,
                                             [bPS[pi]], [bSZ])
                      if stop == 5:
                          raise _Stop()
                      for c0 in range(0, 12, 2):
                          pr = [(c0, 0), (c0 + 1, 1)]
                          for (c, s) in pr:
                              for k in range(8):
                                  S.op(PE, lambda h, k=k: h.matmul(bank(4 + s), lhsT=wssd[:, k, 1024 + c * 128:1024 + (c + 1) * 128],
                                                                   rhs=HT[:, k, :], start=(k == 0), stop=(k == 7)),
                                       reads=[bHT, bwssd], writes=[bPS[4 + s]])
                          for (c, s) in pr:
                              S.op(DVE, lambda h: h.tensor_copy(out=CB[s][:, 0:3], in_=TAIL[:, c, :]), reads=[bTAILc[c]], writes=[bCB[s]])
                              S.op(ACT, lambda h: h.activation(out=CB[s][:, 3:515], in_=bank(4 + s), func=AF.Copy),
                                   reads=[bPS[4 + s]], writes=[bCB[s]])
                          for (c, s) in pr:
                              S.op(DVE, lambda h: h.tensor_copy(out=TAIL[:, c, :], in_=CB[s][:, 512:515]), reads=[bCB[s]],
                                   writes=[bTAILc[c]])
                          for (c, s) in pr:
                              S.op(DVE, lambda h: h.tensor_scalar(out=CA[s][:, :], in0=CB[s][:, 0:512],
                                                                  scalar1=CONVW[:, c * 4:c * 4 + 1], scalar2=CONVB[:, c:c + 1],
                                                                  op0=ALU.mult, op1=ALU.add), reads=[bCB[s], bconst], writes=[bCA[s]])
                          for i in range(1, 4):
                              for (c, s) in pr:
                                  S.op(DVE, lambda h: h.scalar_tensor_tensor(
                                      out=CA[s][:, :], in0=CB[s][:, i:i + 512], scalar=CONVW[:, c * 4 + i:c * 4 + i + 1],
                                      in1=CA[s][:, :], op0=ALU.mult, op1=ALU.add), reads=[bCB[s], bCA[s], bconst], writes=[bCA[s]])
                          for (c, s) in pr:
                              S.op(ACT, lambda h: h.activation(out=CB[s][:, 0:512], in_=CA[s][:, :], func=AF.Exp, scale=-1.0),
                                   reads=[bCA[s]], writes=[bCB[s]])
                          for (c, s) in pr:
                              S.op(ACT, lambda h: h.activation(out=CB[s][:, 0:512], in_=CB[s][:, 0:512], func=AF.Ln, bias=1.0),
                                   reads=[bCB[s]], writes=[bCB[s]])
                          for (c, s) in pr:
                              S.op(ACT, lambda h: h.activation(out=CB[s][:, 0:512], in_=CB[s][:, 0:512], func=AF.Exp, scale=-1.0),
                                   reads=[bCB[s]], writes=[bCB[s]])
                          for (c, s) in pr:
                              S.op(DVE, lambda h: h.tensor_tensor(out=XC[:, c, :], in0=CA[s][:, :], in1=CB[s][:, 0:512], op=ALU.mult),
                                   reads=[bCA[s], bCB[s]], writes=[bXC])
                      if stop == 6:
                          raise _Stop()
                      pd = bank(6)
                      for bb in range(4):
                          for k in range(8):
                              S.op(PE, lambda h, k=k: h.matmul(pd[:, bb * 16:(bb + 1) * 16], lhsT=HT[:, k, bb * 128:(bb + 1) * 128],
                                                               rhs=wssd[:, k, 2560:2576], start=(k == 0), stop=(k == 7)),
                                   reads=[bHT, bwssd], writes=[bPS[6]])
                      S.op(DVE, lambda h: h.tensor_tensor(out=DT[:, :].rearrange("p (a b) -> p a b", a=4), in0=pd[:, 0:64].rearrange("p (a b) -> p a b", a=4),
                                                          in1=DTB.unsqueeze(1).to_broadcast([128, 4, 16]), op=ALU.add),
                           reads=[bPS[6], bconst], writes=[bDT])
                      S.op(ACT, lambda h: h.activation(out=DT[:, :], in_=DT[:, :], func=AF.Exp), reads=[bDT], writes=[bDT])
                      S.op(ACT, lambda h: h.activation(out=DT[:, :], in_=DT[:, :], func=AF.Ln, bias=1.0), reads=[bDT], writes=[bDT])
                      S.op(DVE, lambda h: h.tensor_tensor(out=DA[:, :].rearrange("p (a b) -> p a b", a=4), in0=DT[:, :].rearrange("p (a b) -> p a b", a=4),
                                                          in1=aneg[:, :].unsqueeze(1).to_broadcast([128, 4, 16]), op=ALU.mult),
                           reads=[bDT, bconst], writes=[bDT])

                      S.dma(ldg, GT[:, 0:8, :], gscr[:, tt * 512:(tt + 1) * 512].rearrange("(c p) t -> p c t", p=128),
                            reads=bgscr2, writes=[bGT])

                      if stop == 7:
                          raise _Stop()
                      def front(bb):
                          cs = slice(bb * 128, (bb + 1) * 128)
                          pxs = bank_bf(0)
                          for c in range(8):
                              S.op(PE, lambda h, c=c: h.transpose(out=pxs[:, c * 128:(c + 1) * 128], in_=XC[:, c, cs], identity=IDENT),
                                   reads=[bXC, bconst], writes=[bPS[0]])
                          S.op(ACT, lambda h: h.activation(out=XS_[bb % 2][:, :], in_=pxs, func=AF.Copy), reads=[bPS[0]], writes=[bXS_[bb % 2]])
                          S.op(DVE, lambda h: h.tensor_tensor(out=XDT_[bb % 2][:, :].rearrange("p (a b) -> p a b", a=16),
                                                              in0=pxs.rearrange("p (a b) -> p a b", a=16),
                                                              in1=DT[:, bb * 16:(bb + 1) * 16].unsqueeze(2).to_broadcast([128, 16, 64]), op=ALU.mult),
                               reads=[bPS[0], bDT], writes=[bXDT_[bb % 2]])
                          pbt = bank_bf(1)
                          for g in range(2):
                              S.op(PE, lambda h, g=g: h.transpose(out=pbt[:, g * 128:(g + 1) * 128], in_=XC[:, 8 + g, cs], identity=IDENT),
                                   reads=[bXC, bconst], writes=[bPS[1]])
                          S.op(ACT, lambda h: h.activation(out=BT_[bb % 2][:, :], in_=pbt[:, 0:256], func=AF.Copy), reads=[bPS[1]], writes=[bBT_[bb % 2]])
                          pa = bank(2)
                          S.op(PE, lambda h: h.matmul(pa[:, 0:16], lhsT=TRI_LE, rhs=DA[:, bb * 16:(bb + 1) * 16], start=True, stop=True),
                               reads=[bDT, bconst], writes=[bPS[2]])
                          S.op(PE, lambda h: h.matmul(pa[:, 16:32], lhsT=ONES, rhs=DA[:, bb * 16:(bb + 1) * 16], start=True, stop=True),
                               reads=[bDT, bconst], writes=[bPS[2]])
                          S.op(ACT, lambda h: h.activation(out=EA_[bb % 2][:, :], in_=pa[:, 0:32], func=AF.Exp), reads=[bPS[2]], writes=[bEA_[bb % 2]])
                          S.op(POOL, lambda h: h.tensor_tensor(
                              out=RR[:, :].rearrange("p (a b) -> p a b", a=16), in0=TRI_LE.unsqueeze(1).to_broadcast([128, 16, 128]),
                              in1=DA[:, bb * 16:(bb + 1) * 16].unsqueeze(2).to_broadcast([128, 16, 128]), op=ALU.mult),
                              reads=[bDT, bconst], writes=[bRR])
                          for q4 in range(4):
                              S.op(PE, lambda h, q4=q4: h.matmul(bank(4 + q4), lhsT=LT_STRICT,
                                                                 rhs=RR[:, q4 * 512:(q4 + 1) * 512],
                                                                 start=True, stop=True), reads=[bRR, bconst], writes=[bPS[4 + q4]])
                          S.op(ACT, lambda h: h.activation(out=ET[:, 0:1024], in_=bank2(4), func=AF.Exp),
                               reads=[bPS[4], bPS[5]], writes=[bET])
                          S.op(ACT, lambda h: h.activation(out=ET[:, 1024:2048], in_=bank2(6), func=AF.Exp),
                               reads=[bPS[6], bPS[7]], writes=[bET])
                          pc = bank(3)
                          for g in range(2):
                              S.op(PE, lambda h, g=g: h.matmul(pc[:, g * 128:(g + 1) * 128], lhsT=XC[:, 8 + g, cs], rhs=XC[:, 10 + g, cs],
                                                               start=True, stop=True), reads=[bXC], writes=[bPS[3]])
                          S.op(DVE, lambda h: h.tensor_tensor(out=CBM[:, :].rearrange("p (a b) -> p a b", a=2),
                                                              in0=pc[:, 0:256].rearrange("p (a b) -> p a b", a=2),
                                                              in1=TRI_LE.unsqueeze(1).to_broadcast([128, 2, 128]), op=ALU.mult),
                               reads=[bPS[3], bconst], writes=[bCBM])
                          for g in range(2):
                              S.op(DVE, lambda h, g=g: h.tensor_tensor(
                                  out=MT_[bb % 2][:, g * 1024:(g + 1) * 1024].rearrange("p (a b) -> p a b", a=8),
                                  in0=ET[:, g * 1024:(g + 1) * 1024].rearrange("p (a b) -> p a b", a=8),
                                  in1=CBM[:, g * 128:(g + 1) * 128].unsqueeze(1).to_broadcast([128, 8, 128]), op=ALU.mult),
                                  reads=[bET, bCBM], writes=[bMT_[bb % 2]])
                          S.op(DVE, lambda h: h.tensor_tensor(
                              out=XD_[bb % 2][:, :].rearrange("p (a b) -> p a b", a=16), in0=XDT_[bb % 2][:, :].rearrange("p (a b) -> p a b", a=16),
                              in1=ET[:, :].rearrange("p (a b) -> p a b", a=16)[:, :, 127:128].to_broadcast([128, 16, 64]), op=ALU.mult),
                              reads=[bXDT_[bb % 2], bET], writes=[bXD_[bb % 2]])

                      def back(bb):
                          cs = slice(bb * 128, (bb + 1) * 128)
                          py = bank2(2)
                          S.op(POOL, lambda h: h.tensor_tensor(out=XSD[:, :].rearrange("p (a b) -> p a b", a=16),
                                                               in0=XS_[bb % 2][:, :].rearrange("p (a b) -> p a b", a=16),
                                                               in1=DSK.unsqueeze(2).to_broadcast([128, 16, 64]), op=ALU.mult),
                               reads=[bXS_[bb % 2], bconst], writes=[bXSD])
                          for g in range(2):
                              S.op(PE, lambda h, g=g: h.matmul(py[:, g * 512:(g + 1) * 512], lhsT=IDENT, rhs=XSD[:, g * 512:(g + 1) * 512],
                                                               start=True, stop=False, skip_group_check=True),
                                   reads=[bXSD, bconst], writes=[bPS[2 + g]])
                          for hd in range(16):
                              S.op(PE, lambda h, hd=hd: h.matmul(py[:, hd * 64:(hd + 1) * 64], lhsT=MT_[bb % 2][:, hd * 128:(hd + 1) * 128],
                                                                 rhs=XDT_[bb % 2][:, hd * 64:(hd + 1) * 64], start=False, stop=True,
                                                                 skip_group_check=True),
                                   reads=[bMT_[bb % 2], bXDT_[bb % 2]], writes=[bPS[2], bPS[3]])
                          pyo = bank2(4)
                          for g in range(2):
                              S.op(PE, lambda h, g=g: h.matmul(pyo[:, g * 512:(g + 1) * 512], lhsT=XC[:, 10 + g, cs],
                                                               rhs=STB[:, g * 512:(g + 1) * 512], start=True, stop=True),
                                   reads=[bXC, bSTB], writes=[bPS[4 + g]])
                          S.op(DVE, lambda h: h.tensor_tensor(out=T1[:, :].rearrange("p (a b) -> p a b", a=16),
                                                              in0=pyo.rearrange("p (a b) -> p a b", a=16),
                                                              in1=EA_[bb % 2][:, 0:16].unsqueeze(2).to_broadcast([128, 16, 64]), op=ALU.mult),
                               reads=[bPS[4], bPS[5], bEA_[bb % 2]], writes=[bT1])
                          S.op(DVE, lambda h: h.tensor_tensor(out=T1[:, :], in0=py, in1=T1[:, :], op=ALU.add),
                               reads=[bPS[2], bPS[3], bT1], writes=[bT1])
                          S.op(DVE, lambda h: h.tensor_tensor(out=T1[:, :], in0=T1[:, :], in1=SZ[:, bb, :], op=ALU.mult),
                               reads=[bT1, bSZ], writes=[bT1])
                          S.op(ACT, lambda h: h.activation(out=T2[:, :], in_=T1[:, :], func=AF.Square, accum_out=SSQ2[:, bb:bb + 1]),
                               reads=[bT1, bT2], writes=[bT2, bSSQ2])
                          S.op(ACT, lambda h: h.activation(out=GB[:, :], in_=T1[:, :], func=AF.Copy), reads=[bT1], writes=[bGB])
                          pg = bank_bf(0)
                          for c in range(8):
                              S.op(PE, lambda h, c=c: h.transpose(out=pg[:, c * 128:(c + 1) * 128], in_=GB[:, c * 128:(c + 1) * 128],
                                                                  identity=IDENT), reads=[bGB, bconst], writes=[bPS[0]])
                          S.op(DVE, lambda h: h.tensor_tensor(out=GT[:, 8:16, cs], in0=pg.rearrange("p (a b) -> p a b", a=8),
                                                              in1=SSDW.unsqueeze(2).to_broadcast([128, 8, 128]), op=ALU.mult),
                               reads=[bPS[0], bconst], writes=[bGT])
                          pst = bank2(6)
                          for g in range(2):
                              S.op(PE, lambda h, g=g: h.matmul(pst[:, g * 512:(g + 1) * 512], lhsT=BT_[bb % 2][:, g * 128:(g + 1) * 128],
                                                               rhs=XD_[bb % 2][:, g * 512:(g + 1) * 512], start=True, stop=True),
                                   reads=[bBT_[bb % 2], bXD_[bb % 2]], writes=[bPS[6 + g]])
                          S.op(DVE, lambda h: h.tensor_tensor(out=ST[:, :].rearrange("p (a b) -> p a b", a=16),
                                                              in0=ST[:, :].rearrange("p (a b) -> p a b", a=16),
                                                              in1=EA_[bb % 2][:, 16:32].unsqueeze(2).to_broadcast([128, 16, 64]), op=ALU.mult),
                               reads=[bST, bEA_[bb % 2]], writes=[bST])
                          S.op(DVE, lambda h: h.tensor_tensor(out=ST[:, :], in0=pst, in1=ST[:, :], op=ALU.add),
                               reads=[bPS[6], bPS[7], bST], writes=[bST])
                          S.op(ACT, lambda h: h.activation(out=STB[:, :], in_=ST[:, :], func=AF.Copy), reads=[bST], writes=[bSTB])


                      front(0)
                      for bb in range(4):
                          if bb + 1 < 4:
                              front(bb + 1)
                          back(bb)
                      if stop == 14:
                          raise _Stop()
                      S.op(ACT, lambda h: h.activation(out=SSQ2[:, 4:8], in_=SSQ2[:, 0:4], func=AF.Ln, scale=1.0 / 1024, bias=EPS),
                           reads=[bSSQ2], writes=[bSSQ2])
                      S.op(ACT, lambda h: h.activation(out=SSQ2[:, 4:8], in_=SSQ2[:, 4:8], func=AF.Exp, scale=-0.5),
                           reads=[bSSQ2], writes=[bSSQ2])
                      if stop == 15:
                          raise _Stop()
                      for bb in range(4):
                          blk = tt * 4 + bb
                          cs = slice(bb * 128, (bb + 1) * 128)
                          s = xcnt[0] % 2
                          xcnt[0] += 1
                          S.dma(ldx[s], xt[s][:, :], x_d[seq, blk * 128:(blk + 1) * 128, :], writes=[bxt[s]])
                          for n in range(2):
                              p1 = bank(n * 2); p2 = bank(n * 2 + 1)
                              for c in range(8):
                                  S.op(PE, lambda h, c=c: h.matmul(p1, lhsT=GT[:, c, cs], rhs=wout[:, c, n * 512:(n + 1) * 512],
                                                                   start=(c == 0), stop=(c == 7)), reads=[bGT, bwout], writes=[bPS[n * 2]])
                              for c in range(8, 16):
                                  S.op(PE, lambda h, c=c: h.matmul(p2, lhsT=GT[:, c, cs], rhs=wout[:, c, n * 512:(n + 1) * 512],
                                                                   start=(c == 8), stop=(c == 15)), reads=[bGT, bwout],
                                       writes=[bPS[n * 2 + 1]])
                              ns = slice(n * 512, (n + 1) * 512)
                              S.op(DVE, lambda h: h.scalar_tensor_tensor(out=xt[s][:, ns], in0=p1, scalar=RSB[:, blk:blk + 1],
                                                                         in1=xt[s][:, ns], op0=ALU.mult, op1=ALU.add),
                                   reads=[bPS[n * 2], bRSB, bxt[s]], writes=[bxt[s]])
                              S.op(DVE, lambda h: h.scalar_tensor_tensor(out=xt[s][:, ns], in0=p2, scalar=SSQ2[:, 4 + bb:5 + bb],
                                                                         in1=xt[s][:, ns], op0=ALU.mult, op1=ALU.add),
                                   reads=[bPS[n * 2 + 1], bSSQ2, bxt[s]], writes=[bxt[s]])
                          S.dma(sto[s], out_d[seq, blk * 128:(blk + 1) * 128, :], xt[s][:, :], reads=[bxt[s]])
                  S.barrier()
      except _Stop:
        pass
      S.barrier()
      build.stats = (S.ninst, S.nwaits, S.nsem)
    return nc


_CACHE = {}


def _run(x, w_in, w_out, prm, L, nseq, ncores):
    key = (L, nseq)
    if key not in _CACHE:
        _CACHE[key] = build(L, nseq)
    nc = _CACHE[key]
    cf, cb = _consts()
    in_maps = []
    for c in range(ncores):
        in_maps.append({"x": np.ascontiguousarray(x[c * nseq:(c + 1) * nseq]), "w_in": w_in, "w_out": w_out,
                        "prm": prm, "cf": cf, "cb": cb})
    res = run_bass_kernel_spmd(nc, in_maps, core_ids=list(range(ncores)))
    return np.concatenate([r["out"] for r in res.results], axis=0)


def kernel(x, norm_w, w_in, q_norm_w, k_norm_w, conv_w, conv_b, dt_bias, A_log, D_skip, sb_norm_w, ssd_norm_w, w_out):
    f = lambda a: np.ascontiguousarray(np.asarray(a, dtype=np.float32))
    x = f(x)
    prm = _params(f(norm_w)[0], f(q_norm_w)[0], f(k_norm_w)[0], f(conv_w)[0], f(conv_b)[0], f(dt_bias)[0],
                  f(A_log)[0], f(D_skip)[0], f(sb_norm_w)[0], f(ssd_norm_w)[0])
    B, L, _ = x.shape
    nseq = B // NCORES
    out = _run(x, f(w_in)[0], f(w_out)[0], prm, L, nseq, NCORES)
    return out.astype(np.float32)
```

```python
import numpy as np
import ml_dtypes
from contextlib import ExitStack
import concourse.bass as bass
import concourse.mybir as mybir
from concourse.bass_utils import run_bass_kernel_spmd

F32 = mybir.dt.float32
BF16 = mybir.dt.bfloat16
F32R = mybir.dt.float32r
AF = mybir.ActivationFunctionType
ALU = mybir.AluOpType
EPS = 1e-6
NDUMMY = 4
DIN = 6672
NCORES = 8


class Buf:
    __slots__ = ("name", "w", "r", "excl")

    def __init__(self, name, excl=False):
        self.name = name
        self.w = None
        self.r = []
        self.excl = excl


class _Eng:
    def __init__(self, sched, name, handle, is_pe=False):
        self.name = name
        self.h = handle
        self.is_pe = is_pe
        self.seg = 0
        self.cnt = 0
        self.sem = sched.new_sem(f"c_{name}_0")
        self.known = {}


class _DmaSem:
    def __init__(self, sched, name):
        self.name = name
        self.gen = 0
        self.cnt = 0
        self.sem = sched.new_sem(f"d_{name}_0")


class Sched:
    SEG = 12000

    def __init__(self, nc, stack):
        self.nc = nc
        self.stack = stack
        self.nsem = 0
        self.pe = _Eng(self, "pe", nc.tensor, is_pe=True)
        self.act = _Eng(self, "act", nc.scalar)
        self.dve = _Eng(self, "dve", nc.vector)
        self.pool = _Eng(self, "pool", nc.gpsimd)
        self.sp = _Eng(self, "sp", nc.sync)
        self.engs = [self.pe, self.act, self.dve, self.pool, self.sp]
        self.dsems = []
        self.nwaits = 0
        self.ninst = 0

    def new_sem(self, name):
        self.nsem += 1
        return self.stack.enter_context(self.nc.semaphore(name))

    def dma_sem(self, name):
        d = _DmaSem(self, name)
        self.dsems.append(d)
        return d

    def _wait(self, eng, tok):
        kind, obj, sem, val = tok
        if kind == "e" and obj is eng and (eng.is_pe or eng is self.sp):
            return
        key = id(sem)
        if eng.known.get(key, 0) >= val:
            return
        eng.known[key] = val
        self.nwaits += 1
        eng.h.wait_ge(sem, val)

    @staticmethod
    def _deps(reads, writes):
        deps = []
        for b in reads:
            if b.w is not None:
                deps.append(b.w)
        for b in writes:
            if b.w is not None:
                deps.append(b.w)
            deps.extend(b.r)
        return deps

    @staticmethod
    def _mark(tok, reads, writes):
        for b in reads:
            b.r.append(tok)
            if len(b.r) > 64:
                b.r = b.r[-48:]
        for b in writes:
            b.w = tok
            b.r = []

    def op(self, eng, fn, reads=(), writes=(), relax=False):
        ex = [b for b in reads if b.excl]
        if ex:
            writes = list(writes) + [b for b in ex if b not in writes]
            reads = [b for b in reads if not b.excl]
        for d in self._deps(reads, writes):
            if relax and d[0] == "e" and d[1] is eng:
                continue
            self._wait(eng, d)
        if eng.cnt >= self.SEG:
            eng.seg += 1
            eng.cnt = 0
            eng.sem = self.new_sem(f"c_{eng.name}_{eng.seg}")
        eng.cnt += 1
        fn(eng.h).then_inc(eng.sem, 1)
        self.ninst += 1
        tok = ("e", eng, eng.sem, eng.cnt)
        self._mark(tok, reads, writes)
        return tok

    def dma(self, dsem, out, in_, reads=(), writes=(), **kw):
        eng = self.sp
        for d in self._deps(reads, writes):
            self._wait(eng, d)
        if dsem.cnt + 16 > self.SEG:
            dsem.gen += 1
            dsem.cnt = 0
            dsem.sem = self.new_sem(f"d_{dsem.name}_{dsem.gen}")
        dsem.cnt += 16
        eng.h.dma_start(out=out, in_=in_, **kw).then_inc(dsem.sem, 16)
        tok = ("d", dsem, dsem.sem, dsem.cnt)
        self._mark(tok, reads, writes)
        return tok

    def barrier(self):
        toks = []
        for e in self.engs:
            if e.cnt > 0:
                toks.append(("e", e, e.sem, e.cnt))
        for d in self.dsems:
            if d.cnt > 0:
                toks.append(("d", d, d.sem, d.cnt))
        for e in self.engs:
            for t in toks:
                if t[1] is not e:
                    self._wait(e, t)


def _consts():
    p = np.arange(128)[:, None]
    f = np.arange(128)[None, :]
    ident = (p == f).astype(np.float32)
    ntri = -(p >= f).astype(np.float32)
    nundo = -(p < f).astype(np.float32)
    c = np.arange(512)[None, :]
    neg = np.zeros((128, 4, 512), np.float32)
    for r in range(4):
        neg[:, r, :] = np.where(c <= 128 * r + p, -30000.0, 0.0)
    bd64b = ((p // 64) == (f // 64)).astype(np.float32) / 64.0
    cb = np.concatenate([ident, ntri, nundo, bd64b, neg.reshape(128, 2048)], axis=1).astype(ml_dtypes.bfloat16)
    tri_le = (p <= f).astype(np.float32)
    lt_strict = (p > f).astype(np.float32)
    ones = np.ones((128, 128), np.float32)
    bd64 = ((p // 64) == (f // 64)).astype(np.float32) / 64.0
    cf = np.concatenate([tri_le, lt_strict, ones, bd64], axis=1).astype(np.float32)
    return cf, cb


NPRM = 8 + 2 + 48 + 12 + 48 + 8 + 8


def _params(norm_w, q_norm_w, k_norm_w, conv_w, conv_b, dt_bias, A_log, D_skip, sb_norm_w, ssd_norm_w):
    prm = np.zeros((128, NPRM), np.float32)
    o = 0
    prm[:, o:o + 8] = norm_w.reshape(8, 128).T; o += 8
    prm[:, o] = np.tile(q_norm_w.reshape(64), 2); o += 1
    prm[:, o] = np.tile(k_norm_w.reshape(64), 2); o += 1
    cw = conv_w.reshape(4, 12, 128)
    prm[:, o:o + 48] = cw.transpose(2, 1, 0).reshape(128, 48); o += 48
    prm[:, o:o + 12] = conv_b.reshape(12, 128).T; o += 12
    prm[:, o:o + 16] = np.broadcast_to(dt_bias.reshape(1, 16), (128, 16)); o += 16
    prm[:, o:o + 16] = np.broadcast_to(A_log.reshape(1, 16), (128, 16)); o += 16
    prm[:, o:o + 16] = np.broadcast_to(D_skip.reshape(1, 16), (128, 16)); o += 16
    prm[:, o:o + 8] = sb_norm_w.reshape(8, 128).T; o += 8
    prm[:, o:o + 8] = ssd_norm_w.reshape(8, 128).T; o += 8
    assert o == NPRM
    return prm


class _Stop(Exception):
    pass


_STOPPED = [False]


class _Phase(ExitStack):
    def __exit__(self, et, ev, tb):
        super().__exit__(None, None, None)
        if et is _Stop:
            _STOPPED[0] = True
            return True
        return False


def build(L, NSEQ, stop=None):
    NB = L // 128
    NT = L // 512
    nc = bass.Bass("TRN2", target_bir_lowering=False)
    x_d = nc.dram_tensor("x", [NSEQ, L, 1024], F32, kind="ExternalInput").ap()
    win_d = nc.dram_tensor("w_in", [1024, DIN], F32, kind="ExternalInput").ap()
    wout_d = nc.dram_tensor("w_out", [2048, 1024], F32, kind="ExternalInput").ap()
    prm_d = nc.dram_tensor("prm", [128, NPRM], F32, kind="ExternalInput").ap()
    cf_d = nc.dram_tensor("cf", [128, 512], F32, kind="ExternalInput").ap()
    cb_d = nc.dram_tensor("cb", [128, 512 + 2048], BF16, kind="ExternalInput").ap()
    out_d = nc.dram_tensor("out", [NSEQ, L, 1024], F32, kind="ExternalOutput").ap()
    gscr = nc.dram_tensor("gscr", [1024, L], BF16, kind="Internal").ap()

    with ExitStack() as top:
      S = Sched(nc, top)
      _STOPPED[0] = False
      try:
          PE, ACT, DVE, POOL = S.pe, S.act, S.dve, S.pool

          ucnt = [0]

          def sbt(stk, name, shape, dt):
              ucnt[0] += 1
              return stk.enter_context(nc.sbuf_tensor(f"s{ucnt[0]}_{name}", shape, dt))

          PS = [top.enter_context(nc.psum_tensor(f"ps{i}", [128, 1024], F32)) for i in range(4)]
          bPS = [Buf(f"bank{i}", excl=True) for i in range(8)]

          def bank(i):
              return PS[i // 2][:, (i % 2) * 512:(i % 2) * 512 + 512]

          def bank_bf(i):
              return bank(i).bitcast(BF16)

          def bank2(i):
              return PS[i // 2][:, :]

          prm = sbt(top, "prm", [128, NPRM], F32)
          cf = sbt(top, "cf", [128, 512], F32)
          cb = sbt(top, "cbf", [128, 512 + 2048], BF16)
          qw8 = sbt(top, "qw8", [128, 1], F32)
          aneg = sbt(top, "aneg", [128, 16], F32)
          RSB = sbt(top, "rsb", [128, NB], F32)
          SSQ = sbt(top, "ssq", [128, NB], F32)
          xt = [sbt(top, f"xt{i}", [128, 1024], F32) for i in range(2)]
          xs2 = [sbt(top, f"xs_bf{i}", [128, 1024], BF16) for i in range(2)]; bxs2 = [Buf("xs0"), Buf("xs1")]
          nst2 = [sbt(top, f"nst{i}", [128, 4], F32) for i in range(2)]; bnst2 = [Buf("nst0"), Buf("nst1")]
          wst = [sbt(top, f"wst{i}", [128, 1024], F32) for i in range(2)]
          bconst = Buf("const"); bxt = [Buf("xt0"), Buf("xt1")]; bxs = Buf("xs"); bjunk = Buf("junk")
          bnst = Buf("nst"); bwst = [Buf(f"wst{i}") for i in range(2)]
          bRSB = Buf("rsb"); bSSQ = Buf("ssq"); bgscr = Buf("gscr"); bgscr2 = [Buf("gscr0"), Buf("gscr1")]

          ld = S.dma_sem("ld")
          ldx = [S.dma_sem("ldx0"), S.dma_sem("ldx1")]
          ldw = [S.dma_sem("ldw0"), S.dma_sem("ldw1")]
          stg = [S.dma_sem("stg0"), S.dma_sem("stg1")]
          ldg = S.dma_sem("ldg")
          sto = [S.dma_sem("sto0"), S.dma_sem("sto1")]

          S.dma(ld, prm[:, :], prm_d, writes=[bconst])
          S.dma(ld, cf[:, :], cf_d, writes=[bconst])
          S.dma(ld, cb[:, :], cb_d, writes=[bconst])
          if stop == 0:
              raise _Stop()

          o = 0
          NORMW = prm[:, o:o + 8]; o += 8
          QW = prm[:, o:o + 1]; o += 1
          KW = prm[:, o:o + 1]; o += 1
          CONVW = prm[:, o:o + 48]; o += 48
          CONVB = prm[:, o:o + 12]; o += 12
          DTB = prm[:, o:o + 16]; o += 16
          ALOG = prm[:, o:o + 16]; o += 16
          DSK = prm[:, o:o + 16]; o += 16
          SBW = prm[:, o:o + 8]; o += 8
          SSDW = prm[:, o:o + 8]; o += 8
          TRI_LE = cf[:, 0:128]; LT_STRICT = cf[:, 128:256]; ONES = cf[:, 256:384]; BD64 = cf[:, 384:512]
          IDENT = cb[:, 0:128]; NTRI = cb[:, 128:256]; NUNDO = cb[:, 256:384]; BD64B = cb[:, 384:512]

          def NEG(r):
              return cb[:, 512 + r * 512:512 + (r + 1) * 512]

          S.op(DVE, lambda h: h.tensor_scalar(out=qw8[:, :], in0=QW, scalar1=0.125, scalar2=None, op0=ALU.mult),
               reads=[bconst], writes=[bconst])
          S.op(ACT, lambda h: h.activation(out=aneg[:, :], in_=ALOG, func=AF.Exp), reads=[bconst], writes=[bconst])
          S.op(DVE, lambda h: h.tensor_scalar(out=aneg[:, :], in0=aneg[:, :], scalar1=-1.0, scalar2=None, op0=ALU.mult),
               reads=[bconst], writes=[bconst])

          xcnt = [0]

          def norm_blocks(seq, blks, dst_fn, bdst):
              slots = []

              def stats(i):
                  blk = blks[i]
                  s = xcnt[0] % 2
                  xcnt[0] += 1
                  slots.append(s)
                  S.dma(ldx[s], xt[s][:, :], x_d[seq, blk * 128:(blk + 1) * 128, :], writes=[bxt[s]])
                  S.op(ACT, lambda h: h.activation(out=xs2[s][:, :], in_=xt[s][:, :], func=AF.Square, accum_out=nst2[s][:, 0:1]),
                       reads=[bxt[s]], writes=[bxs2[s], bnst2[s]])
                  S.op(ACT, lambda h: h.activation(out=nst2[s][:, 1:2], in_=nst2[s][:, 0:1], func=AF.Ln, scale=1.0 / 1024, bias=EPS),
                       reads=[bnst2[s]], writes=[bnst2[s]])
                  S.op(ACT, lambda h: h.activation(out=nst2[s][:, 2:3], in_=nst2[s][:, 1:2], func=AF.Exp, scale=-0.5),
                       reads=[bnst2[s]], writes=[bnst2[s]])
                  S.op(DVE, lambda h: h.tensor_scalar(out=xs2[s][:, :], in0=xt[s][:, :], scalar1=nst2[s][:, 2:3], scalar2=None,
                                                      op0=ALU.mult), reads=[bxt[s], bnst2[s]], writes=[bxs2[s]])

              def trans(i):
                  s = slots[i]
                  pb = bank_bf(0)
                  for k in range(8):
                      S.op(PE, lambda h, k=k: h.transpose(out=pb[:, k * 128:(k + 1) * 128], in_=xs2[s][:, k * 128:(k + 1) * 128],
                                                          identity=IDENT), reads=[bxs2[s], bconst], writes=[bPS[0]])
                  S.op(DVE, lambda h: h.tensor_tensor(out=dst_fn(i), in0=pb.rearrange("p (k c) -> p k c", k=8),
                                                      in1=NORMW.unsqueeze(2).to_broadcast([128, 8, 128]), op=ALU.mult),
                       reads=[bPS[0], bconst], writes=[bdst])

              stats(0)
              for i in range(len(blks)):
                  if i + 1 < len(blks):
                      stats(i + 1)
                  trans(i)

          wcnt = [0]

          def load_w(src3, dst3, bdst, shape3):
              s = wcnt[0] % 2
              wcnt[0] += 1
              a, b = shape3
              stv = wst[s][:, 0:a * b].rearrange("p (a b) -> p a b", a=a)
              S.dma(ldw[s], stv, src3, writes=[bwst[s]])
              S.op(DVE, lambda h: h.tensor_copy(out=dst3, in_=stv), reads=[bwst[s]], writes=[bdst])

          def load_w2(src2, dst2, bdst, wd):
              s = wcnt[0] % 2
              wcnt[0] += 1
              stv = wst[s][:, 0:wd]
              S.dma(ldw[s], stv, src2, writes=[bwst[s]])
              if s == 0:
                  S.op(DVE, lambda h: h.tensor_copy(out=dst2, in_=stv), reads=[bwst[s]], writes=[bdst])
              else:
                  S.op(ACT, lambda h: h.activation(out=dst2, in_=stv, func=AF.Copy), reads=[bwst[s]], writes=[bdst])

          def silu_from_psum(pin, dst, tmp, btmp, rd, wr, n=512):
              S.op(ACT, lambda h: h.activation(out=tmp, in_=pin, func=AF.Exp, scale=-1.0), reads=rd, writes=[btmp])
              S.op(ACT, lambda h: h.activation(out=tmp, in_=tmp, func=AF.Ln, bias=1.0), reads=[btmp], writes=[btmp])
              S.op(ACT, lambda h: h.activation(out=tmp, in_=tmp, func=AF.Exp, scale=-1.0), reads=[btmp], writes=[btmp])
              S.op(DVE, lambda h: h.tensor_tensor(out=dst, in0=pin, in1=tmp, op=ALU.mult), reads=rd + [btmp], writes=wr)

          for seq in range(NSEQ):
              with _Phase() as ph:
                  if _STOPPED[0]:
                      raise _Stop()
                  hnT = sbt(ph, "hnT", [128, 8, L], BF16); bhnT = Buf("hnT")
                  QPA = sbt(ph, "QPA", [128, L], BF16); QPB = sbt(ph, "QPB", [128, L], BF16)
                  KT2 = [sbt(ph, f"KT{i}", [128, L], BF16) for i in range(2)]
                  VV2 = [sbt(ph, f"VV{i}", [128, NB * 128], BF16) for i in range(2)]
                  ZS = sbt(ph, "ZS", [128, L], BF16)
                  GP2 = [sbt(ph, f"GP{i}", [128, 512], BF16) for i in range(2)]; bGP2 = [Buf("GP0"), Buf("GP1")]
                  bQ = Buf("Q"); bKT = Buf("KT"); bVV = Buf("VV"); bZS = Buf("ZS"); bGP = Buf("GP")
                  bQt = [Buf(f"Q{i}") for i in range(NT)]; bZSt = [Buf(f"ZS{i}") for i in range(NT)]
                  bKT2 = [[Buf(f"KT{a}_{i}") for i in range(NT)] for a in range(2)]
                  bVV2 = [[Buf(f"VV{a}_{i}") for i in range(NT)] for a in range(2)]
                  gcnt = [0]
                  SQ1 = sbt(ph, "SQ1", [128, 512], BF16); bSQ1 = Buf("SQ1")
                  R1s = sbt(ph, "R1s", [128, 512], F32); bR1s = Buf("R1s")
                  ZT = sbt(ph, "ZT", [128, 512], F32); bZT = Buf("ZT")
                  VTS1 = sbt(ph, "VTS1", [128, 512], BF16); bVTS1 = Buf("VTS1")
                  wbf = [[sbt(ph, f"wbf{a}{i}", [128, 8, 128], BF16) for i in range(4)] for a in range(2)]
                  bwbf = [[Buf(f"wbf{a}{i}") for i in range(4)] for a in range(2)]
                  E = [sbt(ph, f"E{i}", [128, 1024], F32) for i in range(3)]
                  SP = [sbt(ph, f"SP{i}", [128, 1024], BF16) for i in range(2)]
                  WR = [sbt(ph, f"WR{i}", [128, 1024], BF16) for i in range(2)]
                  W = [sbt(ph, f"W{i}", [128, 1024], BF16) for i in range(2)]
                  bE = [Buf("E0"), Buf("E1"), Buf("E2")]; bSP = [Buf("SP0"), Buf("SP1")]
                  bWR = [Buf("WR0"), Buf("WR1")]; bW = [Buf("W0"), Buf("W1")]
                  GF = sbt(ph, "GF", [128, 512], F32); bGF = Buf("GF")
                  GSQ = sbt(ph, "GSQ", [128, 512], F32); bGSQ = Buf("GSQ")

                  S.op(POOL, lambda h: h.memset(QPA[:, :], 0.0), writes=[bQ] + bQt)
                  S.op(POOL, lambda h: h.memset(QPB[:, :], 0.0), writes=[bQ] + bQt)
                  S.op(POOL, lambda h: h.memset(SSQ[:, :], 0.0), writes=[bSSQ])

                  def load_pair_w(hp):
                      a = hp % 2
                      for i in range(4):
                          c0 = i * 1024 + hp * 128
                          load_w(win_d[:, c0:c0 + 128].rearrange("(k p) c -> p k c", p=128), wbf[a][i][:, :, :],
                                 bwbf[a][i], (8, 128))

                  load_pair_w(0)
                  norm_blocks(seq, list(range(NB)), lambda i: hnT[:, :, i * 128:(i + 1) * 128], bhnT)
                  if stop == 1:
                      raise _Stop()

                  qcnt = [0]

                  def qknorm(pb, bpb, dsts, wap):
                      s = qcnt[0] % 2
                      qcnt[0] += 1
                      S.op(ACT, lambda h: h.activation(out=SQ[s][:, :], in_=pb, func=AF.Square), reads=[bpb], writes=[bSQ[s]])
                      pm = bank(1)
                      S.op(PE, lambda h: h.matmul(pm, lhsT=BD64, rhs=SQ[s][:, :], start=True, stop=True),
                           reads=[bSQ[s], bconst], writes=[bPS[1]])
                      S.op(ACT, lambda h: h.activation(out=R1[s][:, :], in_=pm, func=AF.Ln, bias=EPS),
                           reads=[bPS[1]], writes=[bR1[s]])
                      S.op(ACT, lambda h: h.activation(out=R1[s][:, :], in_=R1[s][:, :], func=AF.Exp, scale=-0.5),
                           reads=[bR1[s]], writes=[bR1[s]])
                      for (dst, psl, bd) in dsts:
                          S.op(DVE, lambda h, dst=dst, psl=psl: h.scalar_tensor_tensor(
                              out=dst, in0=pb[psl, :], scalar=wap[psl, :], in1=R1[s][psl, :], op0=ALU.mult, op1=ALU.mult),
                              reads=[bpb, bR1[s], bconst], writes=[bd])

                  for hp in range(8):
                      a = hp % 2
                      KT, VV, bKTt, bVVt = KT2[a], VV2[a], bKT2[a], bVV2[a]
                      def proj_units(tt, wset, bwset, KT, VV, bKTt, bVVt):
                          (wq, wk, wv, wz), (bwq, bwk, bwv, bwz) = wset, bwset
                          ts = slice(tt * 512, (tt + 1) * 512)
                          U = []

                          def mm8(wt, bwt):
                              def half(k0):
                                  def f():
                                      for k in range(k0, k0 + 4):
                                          S.op(PE, lambda h, k=k: h.matmul(bank(2), lhsT=wt[:, k, :], rhs=hnT[:, k, ts],
                                                                           start=(k == 0), stop=(k == 7), skip_group_check=True),
                                               reads=[bwt, bhnT], writes=[bPS[2]])
                                  return f
                              return [half(0), half(4)]

                          def sqms():
                              S.op(ACT, lambda h: h.activation(out=SQ1[:, :], in_=bank(2), func=AF.Square), reads=[bPS[2]], writes=[bSQ1])
                              S.op(PE, lambda h: h.matmul(bank(3), lhsT=BD64B, rhs=SQ1[:, :], start=True, stop=True),
                                   reads=[bSQ1, bconst], writes=[bPS[3]])

                          def ln_():
                              S.op(ACT, lambda h: h.activation(out=R1s[:, :], in_=bank(3), func=AF.Ln, bias=EPS), reads=[bPS[3]], writes=[bR1s])

                          def ex_():
                              S.op(ACT, lambda h: h.activation(out=R1s[:, :], in_=R1s[:, :], func=AF.Exp, scale=-0.5), reads=[bR1s], writes=[bR1s])

                          def fin(dsts, wap):
                              def f():
                                  for (dst, psl, bd) in dsts:
                                      S.op(DVE, lambda h: h.scalar_tensor_tensor(
                                          out=dst, in0=bank(2)[psl, :], scalar=wap[psl, :], in1=R1s[psl, :], op0=ALU.mult, op1=ALU.mult),
                                          reads=[bPS[2], bR1s, bconst], writes=[bd])
                              return f

                          U += mm8(wq, bwq) + [sqms, ln_, ex_,
                                fin([(QPA[0:64, ts], slice(0, 64), bQt[tt]), (QPB[64:128, ts], slice(64, 128), bQt[tt])], qw8)]
                          U += mm8(wk, bwk) + [sqms, ln_, ex_, fin([(KT[:, ts], slice(0, 128), bKTt[tt])], KW)]
                          U += mm8(wz, bwz)
                          U.append(lambda: S.op(ACT, lambda h: h.activation(out=ZT[:, :], in_=bank(2), func=AF.Exp, scale=-1.0),
                                                reads=[bPS[2]], writes=[bZT]))
                          U.append(lambda: S.op(ACT, lambda h: h.activation(out=ZT[:, :], in_=ZT[:, :], func=AF.Ln, bias=1.0),
                                                reads=[bZT], writes=[bZT]))
                          U.append(lambda: S.op(ACT, lambda h: h.activation(out=ZT[:, :], in_=ZT[:, :], func=AF.Exp, scale=-1.0),
                                                reads=[bZT], writes=[bZT]))
                          U.append(lambda: S.op(DVE, lambda h: h.tensor_tensor(out=ZS[:, ts], in0=bank(2), in1=ZT[:, :], op=ALU.mult),
                                                reads=[bPS[2], bZT], writes=[bZSt[tt]]))
                          U += mm8(wv, bwv)
                          U.append(lambda: S.op(DVE, lambda h: h.tensor_copy(out=VTS1[:, :], in_=bank(2)), reads=[bPS[2]], writes=[bVTS1]))

                          def vtr():
                              pvb = bank_bf(3)
                              for b_ in range(4):
                                  S.op(PE, lambda h, b_=b_: h.transpose(out=pvb[:, b_ * 128:(b_ + 1) * 128],
                                                                        in_=VTS1[:, b_ * 128:(b_ + 1) * 128], identity=IDENT),
                                       reads=[bVTS1, bconst], writes=[bPS[3]])
                          U.append(vtr)
                          U.append(lambda: S.op(DVE, lambda h: h.tensor_copy(out=VV[:, ts], in_=bank_bf(3)[:, 0:512]),
                                                reads=[bPS[3]], writes=[bVVt[tt]]))
                          return U

                      NE = min(3, NT)
                      if hp == 0:
                          for tt in range(NE):
                              for u in proj_units(tt, wbf[a], bwbf[a], KT, VV, bKTt, bVVt):
                                  u()
                      if hp + 1 < 8:
                          load_pair_w(hp + 1)
                      queue = []
                      for tt in range(NE, NT):
                          queue += [(tt, 0, u) for u in proj_units(tt, wbf[a], bwbf[a], KT, VV, bKTt, bVVt)]
                      if hp + 1 < 8:
                          an = 1 - a
                          for tt in range(NE):
                              queue += [(NT + tt, tt + 1, u) for u in proj_units(tt, wbf[an], bwbf[an], KT2[an], VV2[an], bKT2[an], bVV2[an])]
                      rate = len(queue) / float(sum(4 * q_ + 4 for q_ in range(NT)))
                      gstep = [0, 0]
                      if stop == 2:
                          raise _Stop()

                      PZ = [bank2(0), bank2(0)]; bPZ = [[bPS[0], bPS[1]], [bPS[0], bPS[1]]]
                      PX = bank2(4); bPX = [bPS[4], bPS[5]]
                      PO = bank2(6); bPO = [bPS[6], bPS[7]]
                      qp = [QPA, QPB]
                      for qi in range(NT):
                          J = 4 * qi + 4
                          steps = list(range(J - 1, -1, -1))
                          qs = slice(qi * 512, (qi + 1) * 512)
                          while queue and queue[0][0] <= qi:
                              queue.pop(0)[2]()
                              gstep[1] += 1

                          nst_ = len(steps)

                          def c0_(i):
                              j = steps[i]
                              return 128 * (j - 4 * qi) if j >= 4 * qi else 0

                          def v2(ap2, i):
                              return ap2.rearrange("p (h c) -> p h c", h=2)[:, :, c0_(i):512]

                          def zmm(i):
                              j = steps[i]; s = i % 2
                              diag = j >= 4 * qi
                              c0 = c0_(i)
                              for hh in range(2):
                                  o_ap = PZ[s][:, hh * 512 + c0:(hh + 1) * 512]
                                  S.op(PE, lambda h: h.matmul(o_ap, lhsT=KT[:, j * 128:(j + 1) * 128],
                                                              rhs=qp[hh][:, qi * 512 + c0:(qi + 1) * 512],
                                                              start=True, stop=not diag),
                                       reads=[bKTt[j // 4], bQt[qi]], writes=[bPZ[s][hh]])
                                  if diag:
                                      S.op(PE, lambda h: h.matmul(o_ap, lhsT=IDENT, rhs=NEG(j - 4 * qi)[:, c0:512], start=False, stop=True),
                                           reads=[bconst], writes=[bPZ[s][hh]])

                          def expz(i):
                              s = i % 2
                              S.op(ACT, lambda h: h.activation(out=v2(E[i % 3][:, :], i), in_=v2(PZ[s], i), func=AF.Exp),
                                   reads=bPZ[s], writes=[bE[i % 3]])

                          def lnsp(i):
                              s = i % 2
                              S.op(ACT, lambda h: h.activation(out=v2(SP[s][:, :], i), in_=v2(E[i % 3][:, :], i), func=AF.Ln, bias=1.0),
                                   reads=[bE[i % 3]], writes=[bSP[s]])

                          def lmm(i):
                              s = i % 2
                              c0 = c0_(i)
                              for hh in range(2):
                                  S.op(PE, lambda h: h.matmul(PX[:, hh * 512 + c0:(hh + 1) * 512], lhsT=NTRI,
                                                              rhs=SP[s][:, hh * 512 + c0:(hh + 1) * 512], start=(i == 0), stop=False,
                                                              skip_group_check=True),
                                       reads=[bSP[s], bconst], writes=[bPX[hh]])

                          def expx(i):
                              s = i % 2
                              S.op(ACT, lambda h: h.activation(out=v2(WR[s][:, :], i), in_=v2(PX, i), func=AF.Exp), reads=bPX, writes=[bWR[s]])

                          def wmul(i):
                              s = i % 2
                              S.op(DVE, lambda h: h.tensor_tensor(out=v2(W[s][:, :], i), in0=v2(E[i % 3][:, :], i), in1=v2(WR[s][:, :], i),
                                                                  op=ALU.mult),
                                   reads=[bE[i % 3], bWR[s]], writes=[bW[s]])

                          def umm(i):
                              s = i % 2
                              c0 = c0_(i)
                              for hh in range(2):
                                  S.op(PE, lambda h: h.matmul(PX[:, hh * 512 + c0:(hh + 1) * 512], lhsT=NUNDO,
                                                              rhs=SP[s][:, hh * 512 + c0:(hh + 1) * 512], start=False, stop=False,
                                                              skip_group_check=True),
                                       reads=[bSP[s], bconst], writes=[bPX[hh]])

                          def pvmm(i):
                              j = steps[i]; s = i % 2
                              c0 = c0_(i)
                              for hh in range(2):
                                  S.op(PE, lambda h: h.matmul(PO[:, hh * 512 + c0:(hh + 1) * 512], lhsT=VV[:, j * 128:(j + 1) * 128],
                                                              rhs=W[s][:, hh * 512 + c0:(hh + 1) * 512], start=(i == 0), stop=(i == nst_ - 1),
                                                              skip_group_check=True),
                                       reads=[bW[s], bVVt[j // 4]], writes=[bPO[hh]])

                          zmm(0)
                          expz(0)
                          lnsp(0)
                          if nst_ > 1:
                              zmm(1)
                          for i in range(nst_):
                              lmm(i)
                              if i + 1 < nst_:
                                  expz(i + 1)
                              expx(i)
                              if i > 0:
                                  pvmm(i - 1)
                              wmul(i)
                              if i + 1 < nst_:
                                  lnsp(i + 1)
                              if i + 2 < nst_:
                                  zmm(i + 2)
                              if i + 1 < nst_:
                                  umm(i)
                              gstep[0] += 1
                              nu = 0
                              while queue and gstep[1] < gstep[0] * rate and queue[0][1] <= qi:
                                  queue.pop(0)[2]()
                                  gstep[1] += 1
                                  nu += 1
                              if nu == 0 and not queue and i + 1 < nst_:
                                  for _d in range(NDUMMY):
                                      S.op(PE, lambda h: h.matmul(bank(2), lhsT=IDENT, rhs=NEG(_d % 4), start=True, stop=True),
                                           reads=[bconst], writes=[bPS[2]])
                          pvmm(nst_ - 1)
                          for hh in range(2):
                              psl = slice(hh * 64, hh * 64 + 64)
                              S.op(DVE, lambda h: h.tensor_tensor(out=GF[psl, :], in0=PO[psl, hh * 512:(hh + 1) * 512],
                                                                  in1=ZS[psl, qs], op=ALU.mult),
                                   reads=[bPO[hh], bZSt[qi]], writes=[bGF])
                          S.op(POOL, lambda h: h.tensor_tensor(out=GSQ[:, :], in0=GF[:, :], in1=GF[:, :], op=ALU.mult),
                               reads=[bGF], writes=[bGSQ])
                          pss = bank(0)
                          for sub in range(4):
                              S.op(PE, lambda h: h.matmul(pss[:, 2 * sub:2 * sub + 2], lhsT=GSQ[:, sub * 128:(sub + 1) * 128],
                                                          rhs=ONES[:, 0:2], start=True, stop=True),
                                   reads=[bGSQ, bconst], writes=[bPS[0]])
                          S.op(DVE, lambda h: h.tensor_tensor(
                              out=SSQ[:, qi * 4:(qi + 1) * 4], in0=pss[:, 0:8].rearrange("p (a b) -> p a b", b=2)[:, :, 0],
                              in1=SSQ[:, qi * 4:(qi + 1) * 4], op=ALU.add), reads=[bPS[0], bSSQ], writes=[bSSQ])
                          g_ = gcnt[0] % 2
                          gcnt[0] += 1
                          S.op(POOL, lambda h: h.tensor_scalar(out=GP2[g_][:, :], in0=GF[:, :], scalar1=SBW[:, hp:hp + 1], scalar2=None,
                                                               op0=ALU.mult), reads=[bGF, bconst], writes=[bGP2[g_]])
                          S.dma(stg[g_], gscr[hp * 128:(hp + 1) * 128, qs], GP2[g_][:, :], reads=[bGP2[g_]], writes=[bgscr2[g_]])
                      while queue:
                          queue.pop(0)[2]()
                      if stop == 3:
                          raise _Stop()

                  S.op(ACT, lambda h: h.activation(out=RSB[:, :], in_=SSQ[:, :], func=AF.Ln, scale=1.0 / 1024, bias=EPS),
                       reads=[bSSQ], writes=[bRSB])
                  S.op(ACT, lambda h: h.activation(out=RSB[:, :], in_=RSB[:, :], func=AF.Exp, scale=-0.5),
                       reads=[bRSB], writes=[bRSB])
                  S.barrier()

              with _Phase() as ph:
                  if _STOPPED[0]:
                      raise _Stop()
                  wssd = sbt(ph, "wssd", [128, 8, 2576], BF16); bwssd = Buf("wssd")
                  wout = sbt(ph, "wout", [128, 16, 1024], BF16); bwout = Buf("wout")
                  HT = sbt(ph, "HT", [128, 8, 512], BF16); bHT = Buf("HT")
                  SZ = sbt(ph, "SZ", [128, 4, 1024], BF16); bSZ = Buf("SZ")
                  XC = sbt(ph, "XC", [128, 12, 512], BF16); bXC = Buf("XC")
                  CB = [sbt(ph, f"CB{i}", [128, 515], F32) for i in range(2)]; bCB = [Buf("CB0"), Buf("CB1")]
                  CA = [sbt(ph, f"CA{i}", [128, 512], F32) for i in range(2)]; bCA = [Buf("CA0"), Buf("CA1")]
                  TAIL = sbt(ph, "TAIL", [128, 12, 3], F32); bTAIL = Buf("TAIL")
                  DT = sbt(ph, "DT", [128, 64], F32); DA = sbt(ph, "DA", [128, 64], F32); bDT = Buf("DT")
                  GT = sbt(ph, "GT", [128, 16, 512], BF16); bGT = Buf("GT")
                  XS_ = [sbt(ph, f"XS{i}", [128, 1024], BF16) for i in range(2)]
                  XDT_ = [sbt(ph, f"XDT{i}", [128, 1024], BF16) for i in range(2)]
                  XD_ = [sbt(ph, f"XD{i}", [128, 1024], BF16) for i in range(2)]
                  BT_ = [sbt(ph, f"BT{i}", [128, 256], BF16) for i in range(2)]
                  EA_ = [sbt(ph, f"EA{i}", [128, 32], F32) for i in range(2)]
                  bXS_ = [Buf("XS0"), Buf("XS1")]; bXDT_ = [Buf("XDT0"), Buf("XDT1")]; bXD_ = [Buf("XD0"), Buf("XD1")]
                  bBT_ = [Buf("BT0"), Buf("BT1")]; bEA_ = [Buf("EA0"), Buf("EA1")]
                  RR = sbt(ph, "RR", [128, 2048], F32); bRR = Buf("RR")
                  ET = sbt(ph, "ET", [128, 2048], F32); bET = Buf("ET")
                  CBM = sbt(ph, "CBM", [128, 256], F32); bCBM = Buf("CBM")
                  MT_ = [sbt(ph, f"MT{i}", [128, 2048], BF16) for i in range(2)]; bMT_ = [Buf("MT0"), Buf("MT1")]
                  T1 = sbt(ph, "T1", [128, 1024], F32); T2 = sbt(ph, "T2", [128, 1024], BF16); XSD = sbt(ph, "XSD", [128, 1024], BF16); bXSD = Buf("XSD")
                  bT1 = Buf("T1"); bT2 = Buf("T2")
                  GB = sbt(ph, "GB", [128, 1024], BF16); bGB = Buf("GB")
                  ST = sbt(ph, "ST", [128, 1024], F32); STB = sbt(ph, "STB", [128, 1024], BF16)
                  bST = Buf("ST"); bSTB = Buf("STB")
                  SSQ2 = sbt(ph, "SSQ2", [128, 8], F32); bSSQ2 = Buf("SSQ2")

                  for k in range(8):
                      for (c0, wd) in ((0, 1024), (1024, 1024), (2048, 528)):
                          load_w2(win_d[k * 128:(k + 1) * 128, 4096 + c0:4096 + c0 + wd], wssd[:, k, c0:c0 + wd], bwssd, wd)
                  for k in range(16):
                      load_w2(wout_d[k * 128:(k + 1) * 128, :], wout[:, k, :], bwout, 1024)
                  S.op(POOL, lambda h: h.memset(ST[:, :], 0.0), writes=[bST])
                  S.op(POOL, lambda h: h.memset(STB[:, :], 0.0), writes=[bSTB])
                  bTAILc = [Buf(f"TAIL{c}") for c in range(12)]
                  S.op(POOL, lambda h: h.memset(TAIL[:, :, :], 0.0), writes=[bTAIL] + bTAILc)
                  if stop == 4:
                      raise _Stop()

                  ccnt = [0]
                  for tt in range(NT):
                      norm_blocks(seq, [tt * 4 + bb for bb in range(4)], lambda i: HT[:, :, i * 128:(i + 1) * 128], bHT)
                      for bb in range(4):
                          pzz = bank2(2)
                          for n in range(2):
                              for k in range(8):
                                  S.op(PE, lambda h, k=k: h.matmul(bank(2 + n), lhsT=HT[:, k, bb * 128:(bb + 1) * 128],
                                                                   rhs=wssd[:, k, n * 512:(n + 1) * 512], start=(k == 0), stop=(k == 7)),
                                       reads=[bHT, bwssd], writes=[bPS[2 + n]])
                          silu_from_psum(pzz, SZ[:, bb, :], T1[:, :], bT1, [bPS[2], bPS[3]], [bSZ])
                      if stop == 5:
                          raise _Stop()
                      for c0 in range(0, 12, 2):
                          pr = [(c0, 0), (c0 + 1, 1)]
                          for (c, s) in pr:
                              for k in range(8):
                                  S.op(PE, lambda h, k=k: h.matmul(bank(4 + s), lhsT=wssd[:, k, 1024 + c * 128:1024 + (c + 1) * 128],
                                                                   rhs=HT[:, k, :], start=(k == 0), stop=(k == 7)),
                                       reads=[bHT, bwssd], writes=[bPS[4 + s]])
                          for (c, s) in pr:
                              S.op(DVE, lambda h: h.tensor_copy(out=CB[s][:, 0:3], in_=TAIL[:, c, :]), reads=[bTAILc[c]], writes=[bCB[s]])
                              S.op(ACT, lambda h: h.activation(out=CB[s][:, 3:515], in_=bank(4 + s), func=AF.Copy),
                                   reads=[bPS[4 + s]], writes=[bCB[s]])
                          for (c, s) in pr:
                              S.op(DVE, lambda h: h.tensor_copy(out=TAIL[:, c, :], in_=CB[s][:, 512:515]), reads=[bCB[s]],
                                   writes=[bTAILc[c]])
                          for (c, s) in pr:
                              S.op(DVE, lambda h: h.tensor_scalar(out=CA[s][:, :], in0=CB[s][:, 0:512],
                                                                  scalar1=CONVW[:, c * 4:c * 4 + 1], scalar2=CONVB[:, c:c + 1],
                                                                  op0=ALU.mult, op1=ALU.add), reads=[bCB[s], bconst], writes=[bCA[s]])
                          for i in range(1, 4):
                              for (c, s) in pr:
                                  S.op(DVE, lambda h: h.scalar_tensor_tensor(
                                      out=CA[s][:, :], in0=CB[s][:, i:i + 512], scalar=CONVW[:, c * 4 + i:c * 4 + i + 1],
                                      in1=CA[s][:, :], op0=ALU.mult, op1=ALU.add), reads=[bCB[s], bCA[s], bconst], writes=[bCA[s]])
                          for (c, s) in pr:
                              S.op(ACT, lambda h: h.activation(out=CB[s][:, 0:512], in_=CA[s][:, :], func=AF.Exp, scale=-1.0),
                                   reads=[bCA[s]], writes=[bCB[s]])
                          for (c, s) in pr:
                              S.op(ACT, lambda h: h.activation(out=CB[s][:, 0:512], in_=CB[s][:, 0:512], func=AF.Ln, bias=1.0),
                                   reads=[bCB[s]], writes=[bCB[s]])
                          for (c, s) in pr:
                              S.op(ACT, lambda h: h.activation(out=CB[s][:, 0:512], in_=CB[s][:, 0:512], func=AF.Exp, scale=-1.0),
                                   reads=[bCB[s]], writes=[bCB[s]])
                          for (c, s) in pr:
                              S.op(DVE, lambda h: h.tensor_tensor(out=XC[:, c, :], in0=CA[s][:, :], in1=CB[s][:, 0:512], op=ALU.mult),
                                   reads=[bCA[s], bCB[s]], writes=[bXC])
                      if stop == 6:
                          raise _Stop()
                      pd = bank(6)
                      for bb in range(4):
                          for k in range(8):
                              S.op(PE, lambda h, k=k: h.matmul(pd[:, bb * 16:(bb + 1) * 16], lhsT=HT[:, k, bb * 128:(bb + 1) * 128],
                                                               rhs=wssd[:, k, 2560:2576], start=(k == 0), stop=(k == 7)),
                                   reads=[bHT, bwssd], writes=[bPS[6]])
                      S.op(DVE, lambda h: h.tensor_tensor(out=DT[:, :].rearrange("p (a b) -> p a b", a=4), in0=pd[:, 0:64].rearrange("p (a b) -> p a b", a=4),
                                                          in1=DTB.unsqueeze(1).to_broadcast([128, 4, 16]), op=ALU.add),
                           reads=[bPS[6], bconst], writes=[bDT])
                      S.op(ACT, lambda h: h.activation(out=DT[:, :], in_=DT[:, :], func=AF.Exp), reads=[bDT], writes=[bDT])
                      S.op(ACT, lambda h: h.activation(out=DT[:, :], in_=DT[:, :], func=AF.Ln, bias=1.0), reads=[bDT], writes=[bDT])
                      S.op(DVE, lambda h: h.tensor_tensor(out=DA[:, :].rearrange("p (a b) -> p a b", a=4), in0=DT[:, :].rearrange("p (a b) -> p a b", a=4),
                                                          in1=aneg[:, :].unsqueeze(1).to_broadcast([128, 4, 16]), op=ALU.mult),
                           reads=[bDT, bconst], writes=[bDT])

                      S.dma(ldg, GT[:, 0:8, :], gscr[:, tt * 512:(tt + 1) * 512].rearrange("(c p) t -> p c t", p=128),
                            reads=bgscr2, writes=[bGT])

                      if stop == 7:
                          raise _Stop()
                      def front(bb):
                          cs = slice(bb * 128, (bb + 1) * 128)
                          pxs = bank_bf(0)
                          for c in range(8):
                              S.op(PE, lambda h, c=c: h.transpose(out=pxs[:, c * 128:(c + 1) * 128], in_=XC[:, c, cs], identity=IDENT),
                                   reads=[bXC, bconst], writes=[bPS[0]])
                          S.op(ACT, lambda h: h.activation(out=XS_[bb % 2][:, :], in_=pxs, func=AF.Copy), reads=[bPS[0]], writes=[bXS_[bb % 2]])
                          S.op(DVE, lambda h: h.tensor_tensor(out=XDT_[bb % 2][:, :].rearrange("p (a b) -> p a b", a=16),
                                                              in0=pxs.rearrange("p (a b) -> p a b", a=16),
                                                              in1=DT[:, bb * 16:(bb + 1) * 16].unsqueeze(2).to_broadcast([128, 16, 64]), op=ALU.mult),
                               reads=[bPS[0], bDT], writes=[bXDT_[bb % 2]])
                          pbt = bank_bf(1)
                          for g in range(2):
                              S.op(PE, lambda h, g=g: h.transpose(out=pbt[:, g * 128:(g + 1) * 128], in_=XC[:, 8 + g, cs], identity=IDENT),
                                   reads=[bXC, bconst], writes=[bPS[1]])
                          S.op(ACT, lambda h: h.activation(out=BT_[bb % 2][:, :], in_=pbt[:, 0:256], func=AF.Copy), reads=[bPS[1]], writes=[bBT_[bb % 2]])
                          pa = bank(2)
                          S.op(PE, lambda h: h.matmul(pa[:, 0:16], lhsT=TRI_LE, rhs=DA[:, bb * 16:(bb + 1) * 16], start=True, stop=True),
                               reads=[bDT, bconst], writes=[bPS[2]])
                          S.op(PE, lambda h: h.matmul(pa[:, 16:32], lhsT=ONES, rhs=DA[:, bb * 16:(bb + 1) * 16], start=True, stop=True),
                               reads=[bDT, bconst], writes=[bPS[2]])
                          S.op(ACT, lambda h: h.activation(out=EA_[bb % 2][:, :], in_=pa[:, 0:32], func=AF.Exp), reads=[bPS[2]], writes=[bEA_[bb % 2]])
                          S.op(POOL, lambda h: h.tensor_tensor(
                              out=RR[:, :].rearrange("p (a b) -> p a b", a=16), in0=TRI_LE.unsqueeze(1).to_broadcast([128, 16, 128]),
                              in1=DA[:, bb * 16:(bb + 1) * 16].unsqueeze(2).to_broadcast([128, 16, 128]), op=ALU.mult),
                              reads=[bDT, bconst], writes=[bRR])
                          for q4 in range(4):
                              S.op(PE, lambda h, q4=q4: h.matmul(bank(4 + q4), lhsT=LT_STRICT,
                                                                 rhs=RR[:, q4 * 512:(q4 + 1) * 512],
                                                                 start=True, stop=True), reads=[bRR, bconst], writes=[bPS[4 + q4]])
                          S.op(ACT, lambda h: h.activation(out=ET[:, 0:1024], in_=bank2(4), func=AF.Exp),
                               reads=[bPS[4], bPS[5]], writes=[bET])
                          S.op(ACT, lambda h: h.activation(out=ET[:, 1024:2048], in_=bank2(6), func=AF.Exp),
                               reads=[bPS[6], bPS[7]], writes=[bET])
                          pc = bank(3)
                          for g in range(2):
                              S.op(PE, lambda h, g=g: h.matmul(pc[:, g * 128:(g + 1) * 128], lhsT=XC[:, 8 + g, cs], rhs=XC[:, 10 + g, cs],
                                                               start=True, stop=True), reads=[bXC], writes=[bPS[3]])
                          S.op(DVE, lambda h: h.tensor_tensor(out=CBM[:, :].rearrange("p (a b) -> p a b", a=2),
                                                              in0=pc[:, 0:256].rearrange("p (a b) -> p a b", a=2),
                                                              in1=TRI_LE.unsqueeze(1).to_broadcast([128, 2, 128]), op=ALU.mult),
                               reads=[bPS[3], bconst], writes=[bCBM])
                          for g in range(2):
                              S.op(DVE, lambda h, g=g: h.tensor_tensor(
                                  out=MT_[bb % 2][:, g * 1024:(g + 1) * 1024].rearrange("p (a b) -> p a b", a=8),
                                  in0=ET[:, g * 1024:(g + 1) * 1024].rearrange("p (a b) -> p a b", a=8),
                                  in1=CBM[:, g * 128:(g + 1) * 128].unsqueeze(1).to_broadcast([128, 8, 128]), op=ALU.mult),
                                  reads=[bET, bCBM], writes=[bMT_[bb % 2]])
                          S.op(DVE, lambda h: h.tensor_tensor(
                              out=XD_[bb % 2][:, :].rearrange("p (a b) -> p a b", a=16), in0=XDT_[bb % 2][:, :].rearrange("p (a b) -> p a b", a=16),
                              in1=ET[:, :].rearrange("p (a b) -> p a b", a=16)[:, :, 127:128].to_broadcast([128, 16, 64]), op=ALU.mult),
                              reads=[bXDT_[bb % 2], bET], writes=[bXD_[bb % 2]])

                      def back(bb):
                          cs = slice(bb * 128, (bb + 1) * 128)
                          py = bank2(2)
                          S.op(POOL, lambda h: h.tensor_tensor(out=XSD[:, :].rearrange("p (a b) -> p a b", a=16),
                                                               in0=XS_[bb % 2][:, :].rearrange("p (a b) -> p a b", a=16),
                                                               in1=DSK.unsqueeze(2).to_broadcast([128, 16, 64]), op=ALU.mult),
                               reads=[bXS_[bb % 2], bconst], writes=[bXSD])
                          for g in range(2):
                              S.op(PE, lambda h, g=g: h.matmul(py[:, g * 512:(g + 1) * 512], lhsT=IDENT, rhs=XSD[:, g * 512:(g + 1) * 512],
                                                               start=True, stop=False, skip_group_check=True),
                                   reads=[bXSD, bconst], writes=[bPS[2 + g]])
                          for hd in range(16):
                              S.op(PE, lambda h, hd=hd: h.matmul(py[:, hd * 64:(hd + 1) * 64], lhsT=MT_[bb % 2][:, hd * 128:(hd + 1) * 128],
                                                                 rhs=XDT_[bb % 2][:, hd * 64:(hd + 1) * 64], start=False, stop=True,
                                                                 skip_group_check=True),
                                   reads=[bMT_[bb % 2], bXDT_[bb % 2]], writes=[bPS[2], bPS[3]])
                          pyo = bank2(4)
                          for g in range(2):
                              S.op(PE, lambda h, g=g: h.matmul(pyo[:, g * 512:(g + 1) * 512], lhsT=XC[:, 10 + g, cs],
                                                               rhs=STB[:, g * 512:(g + 1) * 512], start=True, stop=True),
                                   reads=[bXC, bSTB], writes=[bPS[4 + g]])
                          S.op(DVE, lambda h: h.tensor_tensor(out=T1[:, :].rearrange("p (a b) -> p a b", a=16),
                                                              in0=pyo.rearrange("p (a b) -> p a b", a=16),
                                                              in1=EA_[bb % 2][:, 0:16].unsqueeze(2).to_broadcast([128, 16, 64]), op=ALU.mult),
                               reads=[bPS[4], bPS[5], bEA_[bb % 2]], writes=[bT1])
                          S.op(DVE, lambda h: h.tensor_tensor(out=T1[:, :], in0=py, in1=T1[:, :], op=ALU.add),
                               reads=[bPS[2], bPS[3], bT1], writes=[bT1])
                          S.op(DVE, lambda h: h.tensor_tensor(out=T1[:, :], in0=T1[:, :], in1=SZ[:, bb, :], op=ALU.mult),
                               reads=[bT1, bSZ], writes=[bT1])
                          S.op(ACT, lambda h: h.activation(out=T2[:, :], in_=T1[:, :], func=AF.Square, accum_out=SSQ2[:, bb:bb + 1]),
                               reads=[bT1, bT2], writes=[bT2, bSSQ2])
                          S.op(ACT, lambda h: h.activation(out=GB[:, :], in_=T1[:, :], func=AF.Copy), reads=[bT1], writes=[bGB])
                          pg = bank_bf(0)
                          for c in range(8):
                              S.op(PE, lambda h, c=c: h.transpose(out=pg[:, c * 128:(c + 1) * 128], in_=GB[:, c * 128:(c + 1) * 128],
                                                                  identity=IDENT), reads=[bGB, bconst], writes=[bPS[0]])
                          S.op(DVE, lambda h: h.tensor_tensor(out=GT[:, 8:16, cs], in0=pg.rearrange("p (a b) -> p a b", a=8),
                                                              in1=SSDW.unsqueeze(2).to_broadcast([128, 8, 128]), op=ALU.mult),
                               reads=[bPS[0], bconst], writes=[bGT])
                          pst = bank2(6)
                          for g in range(2):
                              S.op(PE, lambda h, g=g: h.matmul(pst[:, g * 512:(g + 1) * 512], lhsT=BT_[bb % 2][:, g * 128:(g + 1) * 128],
                                                               rhs=XD_[bb % 2][:, g * 512:(g + 1) * 512], start=True, stop=True),
                                   reads=[bBT_[bb % 2], bXD_[bb % 2]], writes=[bPS[6 + g]])
                          S.op(DVE, lambda h: h.tensor_tensor(out=ST[:, :].rearrange("p (a b) -> p a b", a=16),
                                                              in0=ST[:, :].rearrange("p (a b) -> p a b", a=16),
                                                              in1=EA_[bb % 2][:, 16:32].unsqueeze(2).to_broadcast([128, 16, 64]), op=ALU.mult),
                               reads=[bST, bEA_[bb % 2]], writes=[bST])
                          S.op(DVE, lambda h: h.tensor_tensor(out=ST[:, :], in0=pst, in1=ST[:, :], op=ALU.add),
                               reads=[bPS[6], bPS[7], bST], writes=[bST])
                          S.op(ACT, lambda h: h.activation(out=STB[:, :], in_=ST[:, :], func=AF.Copy), reads=[bST], writes=[bSTB])


                      front(0)
                      for bb in range(4):
                          if bb + 1 < 4:
                              front(bb + 1)
                          back(bb)
                      if stop == 14:
                          raise _Stop()
                      S.op(ACT, lambda h: h.activation(out=SSQ2[:, 4:8], in_=SSQ2[:, 0:4], func=AF.Ln, scale=1.0 / 1024, bias=EPS),
                           reads=[bSSQ2], writes=[bSSQ2])
                      S.op(ACT, lambda h: h.activation(out=SSQ2[:, 4:8], in_=SSQ2[:, 4:8], func=AF.Exp, scale=-0.5),
                           reads=[bSSQ2], writes=[bSSQ2])
                      if stop == 15:
                          raise _Stop()
                      for bb in range(4):
                          blk = tt * 4 + bb
                          cs = slice(bb * 128, (bb + 1) * 128)
                          s = xcnt[0] % 2
                          xcnt[0] += 1
                          S.dma(ldx[s], xt[s][:, :], x_d[seq, blk * 128:(blk + 1) * 128, :], writes=[bxt[s]])
                          for n in range(2):
                              p1 = bank(n * 2); p2 = bank(n * 2 + 1)
                              for c in range(8):
                                  S.op(PE, lambda h, c=c: h.matmul(p1, lhsT=GT[:, c, cs], rhs=wout[:, c, n * 512:(n + 1) * 512],
                                                                   start=(c == 0), stop=(c == 7)), reads=[bGT, bwout], writes=[bPS[n * 2]])
                              for c in range(8, 16):
                                  S.op(PE, lambda h, c=c: h.matmul(p2, lhsT=GT[:, c, cs], rhs=wout[:, c, n * 512:(n + 1) * 512],
                                                                   start=(c == 8), stop=(c == 15)), reads=[bGT, bwout],
                                       writes=[bPS[n * 2 + 1]])
                              ns = slice(n * 512, (n + 1) * 512)
                              S.op(DVE, lambda h: h.scalar_tensor_tensor(out=xt[s][:, ns], in0=p1, scalar=RSB[:, blk:blk + 1],
                                                                         in1=xt[s][:, ns], op0=ALU.mult, op1=ALU.add),
                                   reads=[bPS[n * 2], bRSB, bxt[s]], writes=[bxt[s]])
                              S.op(DVE, lambda h: h.scalar_tensor_tensor(out=xt[s][:, ns], in0=p2, scalar=SSQ2[:, 4 + bb:5 + bb],
                                                                         in1=xt[s][:, ns], op0=ALU.mult, op1=ALU.add),
                                   reads=[bPS[n * 2 + 1], bSSQ2, bxt[s]], writes=[bxt[s]])
                          S.dma(sto[s], out_d[seq, blk * 128:(blk + 1) * 128, :], xt[s][:, :], reads=[bxt[s]])
                  S.barrier()
      except _Stop:
        pass
      S.barrier()
      build.stats = (S.ninst, S.nwaits, S.nsem)
    return nc


_CACHE = {}


def _run(x, w_in, w_out, prm, L, nseq, ncores):
    key = (L, nseq)
    if key not in _CACHE:
        _CACHE[key] = build(L, nseq)
    nc = _CACHE[key]
    cf, cb = _consts()
    in_maps = []
    for c in range(ncores):
        in_maps.append({"x": np.ascontiguousarray(x[c * nseq:(c + 1) * nseq]), "w_in": w_in, "w_out": w_out,
                        "prm": prm, "cf": cf, "cb": cb})
    res = run_bass_kernel_spmd(nc, in_maps, core_ids=list(range(ncores)))
    return np.concatenate([r["out"] for r in res.results], axis=0)


def kernel(x, norm_w, w_in, q_norm_w, k_norm_w, conv_w, conv_b, dt_bias, A_log, D_skip, sb_norm_w, ssd_norm_w, w_out):
    f = lambda a: np.ascontiguousarray(np.asarray(a, dtype=np.float32))
    x = f(x)
    prm = _params(f(norm_w)[0], f(q_norm_w)[0], f(k_norm_w)[0], f(conv_w)[0], f(conv_b)[0], f(dt_bias)[0],
                  f(A_log)[0], f(D_skip)[0], f(sb_norm_w)[0], f(ssd_norm_w)[0])
    B, L, _ = x.shape
    nseq = B // NCORES
    out = _run(x, f(w_in)[0], f(w_out)[0], prm, L, nseq, NCORES)
    return out.astype(np.float32)
```

```python
import numpy as np
import ml_dtypes
from contextlib import ExitStack
import concourse.bass as bass
import concourse.mybir as mybir
from concourse.bass_utils import run_bass_kernel_spmd

F32 = mybir.dt.float32
BF16 = mybir.dt.bfloat16
F32R = mybir.dt.float32r
AF = mybir.ActivationFunctionType
ALU = mybir.AluOpType
EPS = 1e-6
NDUMMY = 4
DIN = 6672
NCORES = 8


class Buf:
    __slots__ = ("name", "w", "r", "excl")

    def __init__(self, name, excl=False):
        self.name = name
        self.w = None
        self.r = []
        self.excl = excl


class _Eng:
    def __init__(self, sched, name, handle, is_pe=False):
        self.name = name
        self.h = handle
        self.is_pe = is_pe
        self.seg = 0
        self.cnt = 0
        self.sem = sched.new_sem(f"c_{name}_0")
        self.known = {}


class _DmaSem:
    def __init__(self, sched, name):
        self.name = name
        self.gen = 0
        self.cnt = 0
        self.sem = sched.new_sem(f"d_{name}_0")


class Sched:
    SEG = 12000

    def __init__(self, nc, stack):
        self.nc = nc
        self.stack = stack
        self.nsem = 0
        self.pe = _Eng(self, "pe", nc.tensor, is_pe=True)
        self.act = _Eng(self, "act", nc.scalar)
        self.dve = _Eng(self, "dve", nc.vector)
        self.pool = _Eng(self, "pool", nc.gpsimd)
        self.sp = _Eng(self, "sp", nc.sync)
        self.engs = [self.pe, self.act, self.dve, self.pool, self.sp]
        self.dsems = []
        self.nwaits = 0
        self.ninst = 0

    def new_sem(self, name):
        self.nsem += 1
        return self.stack.enter_context(self.nc.semaphore(name))

    def dma_sem(self, name):
        d = _DmaSem(self, name)
        self.dsems.append(d)
        return d

    def _wait(self, eng, tok):
        kind, obj, sem, val = tok
        if kind == "e" and obj is eng and (eng.is_pe or eng is self.sp):
            return
        key = id(sem)
        if eng.known.get(key, 0) >= val:
            return
        eng.known[key] = val
        self.nwaits += 1
        eng.h.wait_ge(sem, val)

    @staticmethod
    def _deps(reads, writes):
        deps = []
        for b in reads:
            if b.w is not None:
                deps.append(b.w)
        for b in writes:
            if b.w is not None:
                deps.append(b.w)
            deps.extend(b.r)
        return deps

    @staticmethod
    def _mark(tok, reads, writes):
        for b in reads:
            b.r.append(tok)
            if len(b.r) > 64:
                b.r = b.r[-48:]
        for b in writes:
            b.w = tok
            b.r = []

    def op(self, eng, fn, reads=(), writes=(), relax=False):
        ex = [b for b in reads if b.excl]
        if ex:
            writes = list(writes) + [b for b in ex if b not in writes]
            reads = [b for b in reads if not b.excl]
        for d in self._deps(reads, writes):
            if relax and d[0] == "e" and d[1] is eng:
                continue
            self._wait(eng, d)
        if eng.cnt >= self.SEG:
            eng.seg += 1
            eng.cnt = 0
            eng.sem = self.new_sem(f"c_{eng.name}_{eng.seg}")
        eng.cnt += 1
        fn(eng.h).then_inc(eng.sem, 1)
        self.ninst += 1
        tok = ("e", eng, eng.sem, eng.cnt)
        self._mark(tok, reads, writes)
        return tok

    def dma(self, dsem, out, in_, reads=(), writes=(), **kw):
        eng = self.sp
        for d in self._deps(reads, writes):
            self._wait(eng, d)
        if dsem.cnt + 16 > self.SEG:
            dsem.gen += 1
            dsem.cnt = 0
            dsem.sem = self.new_sem(f"d_{dsem.name}_{dsem.gen}")
        dsem.cnt += 16
        eng.h.dma_start(out=out, in_=in_, **kw).then_inc(dsem.sem, 16)
        tok = ("d", dsem, dsem.sem, dsem.cnt)
        self._mark(tok, reads, writes)
        return tok

    def barrier(self):
        toks = []
        for e in self.engs:
            if e.cnt > 0:
                toks.append(("e", e, e.sem, e.cnt))
        for d in self.dsems:
            if d.cnt > 0:
                toks.append(("d", d, d.sem, d.cnt))
        for e in self.engs:
            for t in toks:
                if t[1] is not e:
                    self._wait(e, t)


def _consts():
    p = np.arange(128)[:, None]
    f = np.arange(128)[None, :]
    ident = (p == f).astype(np.float32)
    ntri = -(p >= f).astype(np.float32)
    nundo = -(p < f).astype(np.float32)
    c = np.arange(512)[None, :]
    neg = np.zeros((128, 4, 512), np.float32)
    for r in range(4):
        neg[:, r, :] = np.where(c <= 128 * r + p, -30000.0, 0.0)
    bd64b = ((p // 64) == (f // 64)).astype(np.float32) / 64.0
    cb = np.concatenate([ident, ntri, nundo, bd64b, neg.reshape(128, 2048)], axis=1).astype(ml_dtypes.bfloat16)
    tri_le = (p <= f).astype(np.float32)
    lt_strict = (p > f).astype(np.float32)
    ones = np.ones((128, 128), np.float32)
    bd64 = ((p // 64) == (f // 64)).astype(np.float32) / 64.0
    cf = np.concatenate([tri_le, lt_strict, ones, bd64], axis=1).astype(np.float32)
    return cf, cb


NPRM = 8 + 2 + 48 + 12 + 48 + 8 + 8


def _params(norm_w, q_norm_w, k_norm_w, conv_w, conv_b, dt_bias, A_log, D_skip, sb_norm_w, ssd_norm_w):
    prm = np.zeros((128, NPRM), np.float32)
    o = 0
    prm[:, o:o + 8] = norm_w.reshape(8, 128).T; o += 8
    prm[:, o] = np.tile(q_norm_w.reshape(64), 2); o += 1
    prm[:, o] = np.tile(k_norm_w.reshape(64), 2); o += 1
    cw = conv_w.reshape(4, 12, 128)
    prm[:, o:o + 48] = cw.transpose(2, 1, 0).reshape(128, 48); o += 48
    prm[:, o:o + 12] = conv_b.reshape(12, 128).T; o += 12
    prm[:, o:o + 16] = np.broadcast_to(dt_bias.reshape(1, 16), (128, 16)); o += 16
    prm[:, o:o + 16] = np.broadcast_to(A_log.reshape(1, 16), (128, 16)); o += 16
    prm[:, o:o + 16] = np.broadcast_to(D_skip.reshape(1, 16), (128, 16)); o += 16
    prm[:, o:o + 8] = sb_norm_w.reshape(8, 128).T; o += 8
    prm[:, o:o + 8] = ssd_norm_w.reshape(8, 128).T; o += 8
    assert o == NPRM
    return prm


class _Stop(Exception):
    pass


_STOPPED = [False]


class _Phase(ExitStack):
    def __exit__(self, et, ev, tb):
        super().__exit__(None, None, None)
        if et is _Stop:
            _STOPPED[0] = True
            return True
        return False


def build(L, NSEQ, stop=None):
    NB = L // 128
    NT = L // 512
    nc = bass.Bass("TRN2", target_bir_lowering=False)
    x_d = nc.dram_tensor("x", [NSEQ, L, 1024], F32, kind="ExternalInput").ap()
    win_d = nc.dram_tensor("w_in", [1024, DIN], F32, kind="ExternalInput").ap()
    wout_d = nc.dram_tensor("w_out", [2048, 1024], F32, kind="ExternalInput").ap()
    prm_d = nc.dram_tensor("prm", [128, NPRM], F32, kind="ExternalInput").ap()
    cf_d = nc.dram_tensor("cf", [128, 512], F32, kind="ExternalInput").ap()
    cb_d = nc.dram_tensor("cb", [128, 512 + 2048], BF16, kind="ExternalInput").ap()
    out_d = nc.dram_tensor("out", [NSEQ, L, 1024], F32, kind="ExternalOutput").ap()
    gscr = nc.dram_tensor("gscr", [1024, L], BF16, kind="Internal").ap()

    with ExitStack() as top:
      S = Sched(nc, top)
      _STOPPED[0] = False
      try:
          PE, ACT, DVE, POOL = S.pe, S.act, S.dve, S.pool

          ucnt = [0]

          def sbt(stk, name, shape, dt):
              ucnt[0] += 1
              return stk.enter_context(nc.sbuf_tensor(f"s{ucnt[0]}_{name}", shape, dt))

          PS = [top.enter_context(nc.psum_tensor(f"ps{i}", [128, 1024], F32)) for i in range(4)]
          bPS = [Buf(f"bank{i}", excl=True) for i in range(8)]

          def bank(i):
              return PS[i // 2][:, (i % 2) * 512:(i % 2) * 512 + 512]

          def bank_bf(i):
              return bank(i).bitcast(BF16)

          def bank2(i):
              return PS[i // 2][:, :]

          prm = sbt(top, "prm", [128, NPRM], F32)
          cf = sbt(top, "cf", [128, 512], F32)
          cb = sbt(top, "cbf", [128, 512 + 2048], BF16)
          qw8 = sbt(top, "qw8", [128, 1], F32)
          aneg = sbt(top, "aneg", [128, 16], F32)
          RSB = sbt(top, "rsb", [128, NB], F32)
          SSQ = sbt(top, "ssq", [128, NB], F32)
          xt = [sbt(top, f"xt{i}", [128, 1024], F32) for i in range(2)]
          xs2 = [sbt(top, f"xs_bf{i}", [128, 1024], BF16) for i in range(2)]; bxs2 = [Buf("xs0"), Buf("xs1")]
          nst2 = [sbt(top, f"nst{i}", [128, 4], F32) for i in range(2)]; bnst2 = [Buf("nst0"), Buf("nst1")]
          wst = [sbt(top, f"wst{i}", [128, 1024], F32) for i in range(2)]
          bconst = Buf("const"); bxt = [Buf("xt0"), Buf("xt1")]; bxs = Buf("xs"); bjunk = Buf("junk")
          bnst = Buf("nst"); bwst = [Buf(f"wst{i}") for i in range(2)]
          bRSB = Buf("rsb"); bSSQ = Buf("ssq"); bgscr = Buf("gscr"); bgscr2 = [Buf("gscr0"), Buf("gscr1")]

          ld = S.dma_sem("ld")
          ldx = [S.dma_sem("ldx0"), S.dma_sem("ldx1")]
          ldw = [S.dma_sem("ldw0"), S.dma_sem("ldw1")]
          ldws = [S.dma_sem(f"ldws{i}") for i in range(5)]
          bRRb = Buf("RRb"); bETb = Buf("ETb")
          stg = [S.dma_sem("stg0"), S.dma_sem("stg1")]
          ldg = S.dma_sem("ldg")
          sto = [S.dma_sem("sto0"), S.dma_sem("sto1")]

          S.dma(ld, prm[:, :], prm_d, writes=[bconst])
          S.dma(ld, cf[:, :], cf_d, writes=[bconst])
          S.dma(ld, cb[:, :], cb_d, writes=[bconst])
          if stop == 0:
              raise _Stop()

          o = 0
          NORMW = prm[:, o:o + 8]; o += 8
          QW = prm[:, o:o + 1]; o += 1
          KW = prm[:, o:o + 1]; o += 1
          CONVW = prm[:, o:o + 48]; o += 48
          CONVB = prm[:, o:o + 12]; o += 12
          DTB = prm[:, o:o + 16]; o += 16
          ALOG = prm[:, o:o + 16]; o += 16
          DSK = prm[:, o:o + 16]; o += 16
          SBW = prm[:, o:o + 8]; o += 8
          SSDW = prm[:, o:o + 8]; o += 8
          TRI_LE = cf[:, 0:128]; LT_STRICT = cf[:, 128:256]; ONES = cf[:, 256:384]; BD64 = cf[:, 384:512]
          IDENT = cb[:, 0:128]; NTRI = cb[:, 128:256]; NUNDO = cb[:, 256:384]; BD64B = cb[:, 384:512]

          def NEG(r):
              return cb[:, 512 + r * 512:512 + (r + 1) * 512]

          S.op(DVE, lambda h: h.tensor_scalar(out=qw8[:, :], in0=QW, scalar1=0.125, scalar2=None, op0=ALU.mult),
               reads=[bconst], writes=[bconst])
          S.op(ACT, lambda h: h.activation(out=aneg[:, :], in_=ALOG, func=AF.Exp), reads=[bconst], writes=[bconst])
          S.op(DVE, lambda h: h.tensor_scalar(out=aneg[:, :], in0=aneg[:, :], scalar1=-1.0, scalar2=None, op0=ALU.mult),
               reads=[bconst], writes=[bconst])

          xcnt = [0]

          def norm_blocks(seq, blks, dst_fn, bdst):
              slots = []

              def stats(i):
                  blk = blks[i]
                  s = xcnt[0] % 2
                  xcnt[0] += 1
                  slots.append(s)
                  S.dma(ldx[s], xt[s][:, :], x_d[seq, blk * 128:(blk + 1) * 128, :], writes=[bxt[s]])
                  S.op(ACT, lambda h: h.activation(out=xs2[s][:, :], in_=xt[s][:, :], func=AF.Square, accum_out=nst2[s][:, 0:1]),
                       reads=[bxt[s]], writes=[bxs2[s], bnst2[s]])
                  S.op(ACT, lambda h: h.activation(out=nst2[s][:, 1:2], in_=nst2[s][:, 0:1], func=AF.Ln, scale=1.0 / 1024, bias=EPS),
                       reads=[bnst2[s]], writes=[bnst2[s]])
                  S.op(ACT, lambda h: h.activation(out=nst2[s][:, 2:3], in_=nst2[s][:, 1:2], func=AF.Exp, scale=-0.5),
                       reads=[bnst2[s]], writes=[bnst2[s]])
                  S.op(DVE, lambda h: h.tensor_scalar(out=xs2[s][:, :], in0=xt[s][:, :], scalar1=nst2[s][:, 2:3], scalar2=None,
                                                      op0=ALU.mult), reads=[bxt[s], bnst2[s]], writes=[bxs2[s]])

              def trans(i):
                  s = slots[i]
                  pb = bank_bf(0)
                  for k in range(8):
                      S.op(PE, lambda h, k=k: h.transpose(out=pb[:, k * 128:(k + 1) * 128], in_=xs2[s][:, k * 128:(k + 1) * 128],
                                                          identity=IDENT), reads=[bxs2[s], bconst], writes=[bPS[0]])
                  S.op(DVE, lambda h: h.tensor_tensor(out=dst_fn(i), in0=pb.rearrange("p (k c) -> p k c", k=8),
                                                      in1=NORMW.unsqueeze(2).to_broadcast([128, 8, 128]), op=ALU.mult),
                       reads=[bPS[0], bconst], writes=[bdst])

              stats(0)
              for i in range(len(blks)):
                  if i + 1 < len(blks):
                      stats(i + 1)
                  trans(i)

          wcnt = [0]

          def load_w(src3, dst3, bdst, shape3):
              s = wcnt[0] % 2
              wcnt[0] += 1
              a, b = shape3
              stv = wst[s][:, 0:a * b].rearrange("p (a b) -> p a b", a=a)
              S.dma(ldw[s], stv, src3, writes=[bwst[s]])
              S.op(DVE, lambda h: h.tensor_copy(out=dst3, in_=stv), reads=[bwst[s]], writes=[bdst])

          def silu_from_psum(pin, dst, tmp, btmp, rd, wr, n=512):
              S.op(ACT, lambda h: h.activation(out=tmp, in_=pin, func=AF.Exp, scale=-1.0), reads=rd, writes=[btmp])
              S.op(ACT, lambda h: h.activation(out=tmp, in_=tmp, func=AF.Ln, bias=1.0), reads=[btmp], writes=[btmp])
              S.op(ACT, lambda h: h.activation(out=tmp, in_=tmp, func=AF.Exp, scale=-1.0), reads=[btmp], writes=[btmp])
              S.op(DVE, lambda h: h.tensor_tensor(out=dst, in0=pin, in1=tmp, op=ALU.mult), reads=rd + [btmp], writes=wr)

          for seq in range(NSEQ):
              with _Phase() as ph:
                  if _STOPPED[0]:
                      raise _Stop()
                  hnT = sbt(ph, "hnT", [128, 8, L], BF16); bhnT = Buf("hnT")
                  QPA = sbt(ph, "QPA", [128, L], BF16); QPB = sbt(ph, "QPB", [128, L], BF16)
                  KT2 = [sbt(ph, f"KT{i}", [128, L], BF16) for i in range(2)]
                  VV2 = [sbt(ph, f"VV{i}", [128, NB * 128], BF16) for i in range(2)]
                  ZS = sbt(ph, "ZS", [128, L], BF16)
                  GP2 = [sbt(ph, f"GP{i}", [128, 512], BF16) for i in range(2)]; bGP2 = [Buf("GP0"), Buf("GP1")]
                  bQ = Buf("Q"); bKT = Buf("KT"); bVV = Buf("VV"); bZS = Buf("ZS"); bGP = Buf("GP")
                  bQt = [Buf(f"Q{i}") for i in range(NT)]; bZSt = [Buf(f"ZS{i}") for i in range(NT)]
                  bKT2 = [[Buf(f"KT{a}_{i}") for i in range(NT)] for a in range(2)]
                  bVV2 = [[Buf(f"VV{a}_{i}") for i in range(NT)] for a in range(2)]
                  gcnt = [0]
                  SQ1 = sbt(ph, "SQ1", [128, 512], BF16); bSQ1 = Buf("SQ1")
                  R1s = sbt(ph, "R1s", [128, 512], F32); bR1s = Buf("R1s")
                  ZT = sbt(ph, "ZT", [128, 512], F32); bZT = Buf("ZT")
                  VTS1 = sbt(ph, "VTS1", [128, 512], BF16); bVTS1 = Buf("VTS1")
                  wbf = [[sbt(ph, f"wbf{a}{i}", [128, 8, 128], BF16) for i in range(4)] for a in range(2)]
                  bwbf = [[Buf(f"wbf{a}{i}") for i in range(4)] for a in range(2)]
                  E = [sbt(ph, f"E{i}", [128, 1024], F32) for i in range(3)]
                  SP = [sbt(ph, f"SP{i}", [128, 1024], BF16) for i in range(2)]
                  WR = [sbt(ph, f"WR{i}", [128, 1024], BF16) for i in range(2)]
                  W = [sbt(ph, f"W{i}", [128, 1024], BF16) for i in range(2)]
                  bE = [Buf("E0"), Buf("E1"), Buf("E2")]; bSP = [Buf("SP0"), Buf("SP1")]
                  bWR = [Buf("WR0"), Buf("WR1")]; bW = [Buf("W0"), Buf("W1")]
                  GF = sbt(ph, "GF", [128, 512], F32); bGF = Buf("GF")
                  GSQ = sbt(ph, "GSQ", [128, 512], F32); bGSQ = Buf("GSQ")

                  S.op(POOL, lambda h: h.memset(QPA[:, :], 0.0), writes=[bQ] + bQt)
                  S.op(POOL, lambda h: h.memset(QPB[:, :], 0.0), writes=[bQ] + bQt)
                  S.op(POOL, lambda h: h.memset(SSQ[:, :], 0.0), writes=[bSSQ])

                  def load_pair_w(hp):
                      a = hp % 2
                      for i in range(4):
                          c0 = i * 1024 + hp * 128
                          load_w(win_d[:, c0:c0 + 128].rearrange("(k p) c -> p k c", p=128), wbf[a][i][:, :, :],
                                 bwbf[a][i], (8, 128))

                  load_pair_w(0)
                  norm_blocks(seq, list(range(NB)), lambda i: hnT[:, :, i * 128:(i + 1) * 128], bhnT)
                  if stop == 1:
                      raise _Stop()

                  qcnt = [0]

                  def qknorm(pb, bpb, dsts, wap):
                      s = qcnt[0] % 2
                      qcnt[0] += 1
                      S.op(ACT, lambda h: h.activation(out=SQ[s][:, :], in_=pb, func=AF.Square), reads=[bpb], writes=[bSQ[s]])
                      pm = bank(1)
                      S.op(PE, lambda h: h.matmul(pm, lhsT=BD64, rhs=SQ[s][:, :], start=True, stop=True),
                           reads=[bSQ[s], bconst], writes=[bPS[1]])
                      S.op(ACT, lambda h: h.activation(out=R1[s][:, :], in_=pm, func=AF.Ln, bias=EPS),
                           reads=[bPS[1]], writes=[bR1[s]])
                      S.op(ACT, lambda h: h.activation(out=R1[s][:, :], in_=R1[s][:, :], func=AF.Exp, scale=-0.5),
                           reads=[bR1[s]], writes=[bR1[s]])
                      for (dst, psl, bd) in dsts:
                          S.op(DVE, lambda h, dst=dst, psl=psl: h.scalar_tensor_tensor(
                              out=dst, in0=pb[psl, :], scalar=wap[psl, :], in1=R1[s][psl, :], op0=ALU.mult, op1=ALU.mult),
                              reads=[bpb, bR1[s], bconst], writes=[bd])

                  for hp in range(8):
                      a = hp % 2
                      KT, VV, bKTt, bVVt = KT2[a], VV2[a], bKT2[a], bVV2[a]
                      def proj_units(tt, wset, bwset, KT, VV, bKTt, bVVt):
                          (wq, wk, wv, wz), (bwq, bwk, bwv, bwz) = wset, bwset
                          ts = slice(tt * 512, (tt + 1) * 512)
                          U = []

                          def mm8(wt, bwt):
                              def half(k0):
                                  def f():
                                      for k in range(k0, k0 + 4):
                                          S.op(PE, lambda h, k=k: h.matmul(bank(2), lhsT=wt[:, k, :], rhs=hnT[:, k, ts],
                                                                           start=(k == 0), stop=(k == 7), skip_group_check=True),
                                               reads=[bwt, bhnT], writes=[bPS[2]])
                                  return f
                              return [half(0), half(4)]

                          def sqms():
                              S.op(ACT, lambda h: h.activation(out=SQ1[:, :], in_=bank(2), func=AF.Square), reads=[bPS[2]], writes=[bSQ1])
                              S.op(PE, lambda h: h.matmul(bank(3), lhsT=BD64B, rhs=SQ1[:, :], start=True, stop=True),
                                   reads=[bSQ1, bconst], writes=[bPS[3]])

                          def ln_():
                              S.op(ACT, lambda h: h.activation(out=R1s[:, :], in_=bank(3), func=AF.Ln, bias=EPS), reads=[bPS[3]], writes=[bR1s])

                          def ex_():
                              S.op(ACT, lambda h: h.activation(out=R1s[:, :], in_=R1s[:, :], func=AF.Exp, scale=-0.5), reads=[bR1s], writes=[bR1s])

                          def fin(dsts, wap):
                              def f():
                                  for (dst, psl, bd) in dsts:
                                      S.op(DVE, lambda h: h.scalar_tensor_tensor(
                                          out=dst, in0=bank(2)[psl, :], scalar=wap[psl, :], in1=R1s[psl, :], op0=ALU.mult, op1=ALU.mult),
                                          reads=[bPS[2], bR1s, bconst], writes=[bd])
                              return f

                          U += mm8(wq, bwq) + [sqms, ln_, ex_,
                                fin([(QPA[0:64, ts], slice(0, 64), bQt[tt]), (QPB[64:128, ts], slice(64, 128), bQt[tt])], qw8)]
                          U += mm8(wk, bwk) + [sqms, ln_, ex_, fin([(KT[:, ts], slice(0, 128), bKTt[tt])], KW)]
                          U += mm8(wz, bwz)
                          U.append(lambda: S.op(ACT, lambda h: h.activation(out=ZT[:, :], in_=bank(2), func=AF.Exp, scale=-1.0),
                                                reads=[bPS[2]], writes=[bZT]))
                          U.append(lambda: S.op(ACT, lambda h: h.activation(out=ZT[:, :], in_=ZT[:, :], func=AF.Ln, bias=1.0),
                                                reads=[bZT], writes=[bZT]))
                          U.append(lambda: S.op(ACT, lambda h: h.activation(out=ZT[:, :], in_=ZT[:, :], func=AF.Exp, scale=-1.0),
                                                reads=[bZT], writes=[bZT]))
                          U.append(lambda: S.op(DVE, lambda h: h.tensor_tensor(out=ZS[:, ts], in0=bank(2), in1=ZT[:, :], op=ALU.mult),
                                                reads=[bPS[2], bZT], writes=[bZSt[tt]]))
                          U += mm8(wv, bwv)
                          U.append(lambda: S.op(DVE, lambda h: h.tensor_copy(out=VTS1[:, :], in_=bank(2)), reads=[bPS[2]], writes=[bVTS1]))

                          def vtr():
                              pvb = bank_bf(3)
                              for b_ in range(4):
                                  S.op(PE, lambda h, b_=b_: h.transpose(out=pvb[:, b_ * 128:(b_ + 1) * 128],
                                                                        in_=VTS1[:, b_ * 128:(b_ + 1) * 128], identity=IDENT),
                                       reads=[bVTS1, bconst], writes=[bPS[3]])
                          U.append(vtr)
                          U.append(lambda: S.op(DVE, lambda h: h.tensor_copy(out=VV[:, ts], in_=bank_bf(3)[:, 0:512]),
                                                reads=[bPS[3]], writes=[bVVt[tt]]))
                          return U

                      NE = min(3, NT)
                      if hp == 0:
                          for tt in range(NE):
                              for u in proj_units(tt, wbf[a], bwbf[a], KT, VV, bKTt, bVVt):
                                  u()
                      if hp + 1 < 8:
                          load_pair_w(hp + 1)
                      queue = []
                      for tt in range(NE, NT):
                          queue += [(tt, 0, u) for u in proj_units(tt, wbf[a], bwbf[a], KT, VV, bKTt, bVVt)]
                      if hp + 1 < 8:
                          an = 1 - a
                          for tt in range(NE):
                              queue += [(NT + tt, tt + 1, u) for u in proj_units(tt, wbf[an], bwbf[an], KT2[an], VV2[an], bKT2[an], bVV2[an])]
                      rate = len(queue) / float(sum(4 * q_ + 4 for q_ in range(NT)))
                      gstep = [0, 0]
                      if stop == 2:
                          raise _Stop()

                      PZ = [bank2(0), bank2(0)]; bPZ = [[bPS[0], bPS[1]], [bPS[0], bPS[1]]]
                      PX = bank2(4); bPX = [bPS[4], bPS[5]]
                      PO = bank2(6); bPO = [bPS[6], bPS[7]]
                      qp = [QPA, QPB]
                      for qi in range(NT):
                          J = 4 * qi + 4
                          steps = list(range(J - 1, -1, -1))
                          qs = slice(qi * 512, (qi + 1) * 512)
                          while queue and queue[0][0] <= qi:
                              queue.pop(0)[2]()
                              gstep[1] += 1

                          nst_ = len(steps)

                          def c0_(i):
                              j = steps[i]
                              return 128 * (j - 4 * qi) if j >= 4 * qi else 0

                          def v2(ap2, i):
                              return ap2.rearrange("p (h c) -> p h c", h=2)[:, :, c0_(i):512]

                          def zmm(i):
                              j = steps[i]; s = i % 2
                              diag = j >= 4 * qi
                              c0 = c0_(i)
                              for hh in range(2):
                                  o_ap = PZ[s][:, hh * 512 + c0:(hh + 1) * 512]
                                  S.op(PE, lambda h: h.matmul(o_ap, lhsT=KT[:, j * 128:(j + 1) * 128],
                                                              rhs=qp[hh][:, qi * 512 + c0:(qi + 1) * 512],
                                                              start=True, stop=not diag),
                                       reads=[bKTt[j // 4], bQt[qi]], writes=[bPZ[s][hh]])
                                  if diag:
                                      S.op(PE, lambda h: h.matmul(o_ap, lhsT=IDENT, rhs=NEG(j - 4 * qi)[:, c0:512], start=False, stop=True),
                                           reads=[bconst], writes=[bPZ[s][hh]])

                          def expz(i):
                              s = i % 2
                              S.op(ACT, lambda h: h.activation(out=v2(E[i % 3][:, :], i), in_=v2(PZ[s], i), func=AF.Exp),
                                   reads=bPZ[s], writes=[bE[i % 3]])

                          def lnsp(i):
                              s = i % 2
                              S.op(ACT, lambda h: h.activation(out=v2(SP[s][:, :], i), in_=v2(E[i % 3][:, :], i), func=AF.Ln, bias=1.0),
                                   reads=[bE[i % 3]], writes=[bSP[s]])

                          def lmm(i):
                              s = i % 2
                              c0 = c0_(i)
                              for hh in range(2):
                                  S.op(PE, lambda h: h.matmul(PX[:, hh * 512 + c0:(hh + 1) * 512], lhsT=NTRI,
                                                              rhs=SP[s][:, hh * 512 + c0:(hh + 1) * 512], start=(i == 0), stop=False,
                                                              skip_group_check=True),
                                       reads=[bSP[s], bconst], writes=[bPX[hh]])

                          def expx(i):
                              s = i % 2
                              S.op(ACT, lambda h: h.activation(out=v2(WR[s][:, :], i), in_=v2(PX, i), func=AF.Exp), reads=bPX, writes=[bWR[s]])

                          def wmul(i):
                              s = i % 2
                              S.op(DVE, lambda h: h.tensor_tensor(out=v2(W[s][:, :], i), in0=v2(E[i % 3][:, :], i), in1=v2(WR[s][:, :], i),
                                                                  op=ALU.mult),
                                   reads=[bE[i % 3], bWR[s]], writes=[bW[s]])

                          def umm(i):
                              s = i % 2
                              c0 = c0_(i)
                              for hh in range(2):
                                  S.op(PE, lambda h: h.matmul(PX[:, hh * 512 + c0:(hh + 1) * 512], lhsT=NUNDO,
                                                              rhs=SP[s][:, hh * 512 + c0:(hh + 1) * 512], start=False, stop=False,
                                                              skip_group_check=True),
                                       reads=[bSP[s], bconst], writes=[bPX[hh]])

                          def pvmm(i):
                              j = steps[i]; s = i % 2
                              c0 = c0_(i)
                              for hh in range(2):
                                  S.op(PE, lambda h: h.matmul(PO[:, hh * 512 + c0:(hh + 1) * 512], lhsT=VV[:, j * 128:(j + 1) * 128],
                                                              rhs=W[s][:, hh * 512 + c0:(hh + 1) * 512], start=(i == 0), stop=(i == nst_ - 1),
                                                              skip_group_check=True),
                                       reads=[bW[s], bVVt[j // 4]], writes=[bPO[hh]])

                          zmm(0)
                          expz(0)
                          lnsp(0)
                          if nst_ > 1:
                              zmm(1)
                          for i in range(nst_):
                              lmm(i)
                              if i + 1 < nst_:
                                  expz(i + 1)
                              expx(i)
                              if i > 0:
                                  pvmm(i - 1)
                              wmul(i)
                              if i + 1 < nst_:
                                  lnsp(i + 1)
                              if i + 2 < nst_:
                                  zmm(i + 2)
                              if i + 1 < nst_:
                                  umm(i)
                              gstep[0] += 1
                              nu = 0
                              while queue and gstep[1] < gstep[0] * rate and queue[0][1] <= qi:
                                  queue.pop(0)[2]()
                                  gstep[1] += 1
                                  nu += 1
                              if nu == 0 and not queue and i + 1 < nst_:
                                  for _d in range(NDUMMY):
                                      S.op(PE, lambda h: h.matmul(bank(2), lhsT=IDENT, rhs=NEG(_d % 4), start=True, stop=True),
                                           reads=[bconst], writes=[bPS[2]])
                          pvmm(nst_ - 1)
                          for hh in range(2):
                              psl = slice(hh * 64, hh * 64 + 64)
                              S.op(DVE, lambda h: h.tensor_tensor(out=GF[psl, :], in0=PO[psl, hh * 512:(hh + 1) * 512],
                                                                  in1=ZS[psl, qs], op=ALU.mult),
                                   reads=[bPO[hh], bZSt[qi]], writes=[bGF])
                          S.op(POOL, lambda h: h.tensor_tensor(out=GSQ[:, :], in0=GF[:, :], in1=GF[:, :], op=ALU.mult),
                               reads=[bGF], writes=[bGSQ])
                          pss = bank(0)
                          for sub in range(4):
                              S.op(PE, lambda h: h.matmul(pss[:, 2 * sub:2 * sub + 2], lhsT=GSQ[:, sub * 128:(sub + 1) * 128],
                                                          rhs=ONES[:, 0:2], start=True, stop=True),
                                   reads=[bGSQ, bconst], writes=[bPS[0]])
                          S.op(DVE, lambda h: h.tensor_tensor(
                              out=SSQ[:, qi * 4:(qi + 1) * 4], in0=pss[:, 0:8].rearrange("p (a b) -> p a b", b=2)[:, :, 0],
                              in1=SSQ[:, qi * 4:(qi + 1) * 4], op=ALU.add), reads=[bPS[0], bSSQ], writes=[bSSQ])
                          g_ = gcnt[0] % 2
                          gcnt[0] += 1
                          S.op(POOL, lambda h: h.tensor_scalar(out=GP2[g_][:, :], in0=GF[:, :], scalar1=SBW[:, hp:hp + 1], scalar2=None,
                                                               op0=ALU.mult), reads=[bGF, bconst], writes=[bGP2[g_]])
                          S.dma(stg[g_], gscr[hp * 128:(hp + 1) * 128, qs], GP2[g_][:, :], reads=[bGP2[g_]], writes=[bgscr2[g_]])
                      while queue:
                          queue.pop(0)[2]()
                      if stop == 3:
                          raise _Stop()

                  S.op(ACT, lambda h: h.activation(out=RSB[:, :], in_=SSQ[:, :], func=AF.Ln, scale=1.0 / 1024, bias=EPS),
                       reads=[bSSQ], writes=[bRSB])
                  S.op(ACT, lambda h: h.activation(out=RSB[:, :], in_=RSB[:, :], func=AF.Exp, scale=-0.5),
                       reads=[bRSB], writes=[bRSB])
                  S.barrier()

              with _Phase() as ph:
                  if _STOPPED[0]:
                      raise _Stop()
                  wssd = sbt(ph, "wssd", [128, 8, 2576], BF16); bwssd = Buf("wssd")
                  wout = sbt(ph, "wout", [128, 16, 1024], BF16); bwout = Buf("wout")
                  HT = sbt(ph, "HT", [128, 8, 512], BF16); bHT = Buf("HT")
                  SZ = sbt(ph, "SZ", [128, 4, 1024], BF16); bSZ = Buf("SZ")
                  XC = sbt(ph, "XC", [128, 12, 512], BF16); bXC = Buf("XC")
                  CB = [sbt(ph, f"CB{i}", [128, 515], F32) for i in range(2)]; bCB = [Buf("CB0"), Buf("CB1")]
                  CA = [sbt(ph, f"CA{i}", [128, 512], F32) for i in range(2)]; bCA = [Buf("CA0"), Buf("CA1")]
                  TAIL = sbt(ph, "TAIL", [128, 12, 3], F32); bTAIL = Buf("TAIL")
                  DT = sbt(ph, "DT", [128, 64], F32); DA = sbt(ph, "DA", [128, 64], F32); bDT = Buf("DT")
                  GT = sbt(ph, "GT", [128, 16, 512], BF16); bGT = Buf("GT")
                  XS_ = [sbt(ph, f"XS{i}", [128, 1024], BF16) for i in range(2)]
                  XDT_ = [sbt(ph, f"XDT{i}", [128, 1024], BF16) for i in range(2)]
                  XD_ = [sbt(ph, f"XD{i}", [128, 1024], BF16) for i in range(2)]
                  BT_ = [sbt(ph, f"BT{i}", [128, 256], BF16) for i in range(2)]
                  EA_ = [sbt(ph, f"EA{i}", [128, 32], F32) for i in range(2)]
                  bXS_ = [Buf("XS0"), Buf("XS1")]; bXDT_ = [Buf("XDT0"), Buf("XDT1")]; bXD_ = [Buf("XD0"), Buf("XD1")]
                  bBT_ = [Buf("BT0"), Buf("BT1")]; bEA_ = [Buf("EA0"), Buf("EA1")]
                  RR = sbt(ph, "RR", [128, 2048], F32); bRR = Buf("RR")
                  ET = sbt(ph, "ET", [128, 2048], F32); bET = Buf("ET")
                  CBM = sbt(ph, "CBM", [128, 256], F32); bCBM = Buf("CBM")
                  MT_ = [sbt(ph, f"MT{i}", [128, 2048], BF16) for i in range(2)]; bMT_ = [Buf("MT0"), Buf("MT1")]
                  T1 = sbt(ph, "T1", [128, 1024], F32); T2 = sbt(ph, "T2", [128, 1024], BF16); XSD = sbt(ph, "XSD", [128, 1024], BF16); bXSD = Buf("XSD")
                  bT1 = Buf("T1"); bT2 = Buf("T2")
                  GB = sbt(ph, "GB", [128, 1024], BF16); bGB = Buf("GB")
                  ST = sbt(ph, "ST", [128, 1024], F32); STB = sbt(ph, "STB", [128, 1024], BF16)
                  bST = Buf("ST"); bSTB = Buf("STB")
                  SSQ2 = sbt(ph, "SSQ2", [128, 8], F32); bSSQ2 = Buf("SSQ2")

                  stg_slots = [(wst[0][:, :], bwst[0], ldw[0]), (wst[1][:, :], bwst[1], ldw[1]),
                               (RR[:, 0:1024], bRR, ldws[0]), (RR[:, 1024:2048], bRRb, ldws[1]),
                               (ET[:, 0:1024], bET, ldws[2]), (ET[:, 1024:2048], bETb, ldws[3]), (T1[:, :], bT1, ldws[4])]
                  scnt = [0]

                  def load_ws(src3, dst3, bdst, shape3):
                      (stt, bst, dsm) = stg_slots[scnt[0] % len(stg_slots)]
                      scnt[0] += 1
                      a_, b_ = shape3
                      stv = stt[:, 0:a_ * b_].rearrange("p (a b) -> p a b", a=a_)
                      S.dma(dsm, stv, src3, writes=[bst])
                      S.op(DVE, lambda h: h.tensor_copy(out=dst3, in_=stv), reads=[bst], writes=[bdst])

                  for c in range(20):
                      c0 = 4096 + c * 128
                      load_ws(win_d[:, c0:c0 + 128].rearrange("(k p) c -> p k c", p=128), wssd[:, :, c * 128:(c + 1) * 128],
                              bwssd, (8, 128))
                  load_ws(win_d[:, 6656:6672].rearrange("(k p) c -> p k c", p=128), wssd[:, :, 2560:2576], bwssd, (8, 16))
                  for c in range(16):
                      load_ws(wout_d[:, c * 64:(c + 1) * 64].rearrange("(k p) c -> p k c", p=128), wout[:, :, c * 64:(c + 1) * 64],
                              bwout, (16, 64))
                  S.op(DVE, lambda h: h.memset(SSQ2[:, 0:1], 0.0), reads=[bRRb, bETb], writes=[bRR, bET, bSSQ2])
                  S.op(POOL, lambda h: h.memset(ST[:, :], 0.0), writes=[bST])
                  S.op(POOL, lambda h: h.memset(STB[:, :], 0.0), writes=[bSTB])
                  bTAILc = [Buf(f"TAIL{c}") for c in range(12)]
                  S.op(POOL, lambda h: h.memset(TAIL[:, :, :], 0.0), writes=[bTAIL] + bTAILc)
                  if stop == 4:
                      raise _Stop()

                  ccnt = [0]
                  for tt in range(NT):
                      norm_blocks(seq, [tt * 4 + bb for bb in range(4)], lambda i: HT[:, :, i * 128:(i + 1) * 128], bHT)
                      for bb in range(4):
                          for n in range(2):
                              pi = 2 + ((bb * 2 + n) % 2)
                              pz = bank(pi)
                              for k in range(8):
                                  S.op(PE, lambda h, k=k: h.matmul(pz, lhsT=HT[:, k, bb * 128:(bb + 1) * 128],
                                                                   rhs=wssd[:, k, n * 512:(n + 1) * 512], start=(k == 0), stop=(k == 7)),
                                       reads=[bHT, bwssd], writes=[bPS[pi]])
                              silu_from_psum(pz, SZ[:, bb, n * 512:(n + 1) * 512], T1[:, n * 512:(n + 1) * 512], bT1,
                                             [bPS[pi]], [bSZ])
                      if stop == 5:
                          raise _Stop()
                      for c0 in range(0, 12, 2):
                          pr = [(c0, 0), (c0 + 1, 1)]
                          for (c, s) in pr:
                              for k in range(8):
                                  S.op(PE, lambda h, k=k: h.matmul(bank(4 + s), lhsT=wssd[:, k, 1024 + c * 128:1024 + (c + 1) * 128],
                                                                   rhs=HT[:, k, :], start=(k == 0), stop=(k == 7)),
                                       reads=[bHT, bwssd], writes=[bPS[4 + s]])
                          for (c, s) in pr:
                              S.op(DVE, lambda h: h.tensor_copy(out=CB[s][:, 0:3], in_=TAIL[:, c, :]), reads=[bTAILc[c]], writes=[bCB[s]])
                              S.op(ACT, lambda h: h.activation(out=CB[s][:, 3:515], in_=bank(4 + s), func=AF.Copy),
                                   reads=[bPS[4 + s]], writes=[bCB[s]])
                          for (c, s) in pr:
                              S.op(DVE, lambda h: h.tensor_copy(out=TAIL[:, c, :], in_=CB[s][:, 512:515]), reads=[bCB[s]],
                                   writes=[bTAILc[c]])
                          for (c, s) in pr:
                              S.op(DVE, lambda h: h.tensor_scalar(out=CA[s][:, :], in0=CB[s][:, 0:512],
                                                                  scalar1=CONVW[:, c * 4:c * 4 + 1], scalar2=CONVB[:, c:c + 1],
                                                                  op0=ALU.mult, op1=ALU.add), reads=[bCB[s], bconst], writes=[bCA[s]])
                          for i in range(1, 4):
                              for (c, s) in pr:
                                  S.op(DVE, lambda h: h.scalar_tensor_tensor(
                                      out=CA[s][:, :], in0=CB[s][:, i:i + 512], scalar=CONVW[:, c * 4 + i:c * 4 + i + 1],
                                      in1=CA[s][:, :], op0=ALU.mult, op1=ALU.add), reads=[bCB[s], bCA[s], bconst], writes=[bCA[s]])
                          for (c, s) in pr:
                              S.op(ACT, lambda h: h.activation(out=CB[s][:, 0:512], in_=CA[s][:, :], func=AF.Exp, scale=-1.0),
                                   reads=[bCA[s]], writes=[bCB[s]])
                          for (c, s) in pr:
                              S.op(ACT, lambda h: h.activation(out=CB[s][:, 0:512], in_=CB[s][:, 0:512], func=AF.Ln, bias=1.0),
                                   reads=[bCB[s]], writes=[bCB[s]])
                          for (c, s) in pr:
                              S.op(ACT, lambda h: h.activation(out=CB[s][:, 0:512], in_=CB[s][:, 0:512], func=AF.Exp, scale=-1.0),
                                   reads=[bCB[s]], writes=[bCB[s]])
                          for (c, s) in pr:
                              S.op(DVE, lambda h: h.tensor_tensor(out=XC[:, c, :], in0=CA[s][:, :], in1=CB[s][:, 0:512], op=ALU.mult),
                                   reads=[bCA[s], bCB[s]], writes=[bXC])
                      if stop == 6:
                          raise _Stop()
                      pd = bank(6)
                      for bb in range(4):
                          for k in range(8):
                              S.op(PE, lambda h, k=k: h.matmul(pd[:, bb * 16:(bb + 1) * 16], lhsT=HT[:, k, bb * 128:(bb + 1) * 128],
                                                               rhs=wssd[:, k, 2560:2576], start=(k == 0), stop=(k == 7)),
                                   reads=[bHT, bwssd], writes=[bPS[6]])
                      S.op(DVE, lambda h: h.tensor_tensor(out=DT[:, :].rearrange("p (a b) -> p a b", a=4), in0=pd[:, 0:64].rearrange("p (a b) -> p a b", a=4),
                                                          in1=DTB.unsqueeze(1).to_broadcast([128, 4, 16]), op=ALU.add),
                           reads=[bPS[6], bconst], writes=[bDT])
                      S.op(ACT, lambda h: h.activation(out=DT[:, :], in_=DT[:, :], func=AF.Exp), reads=[bDT], writes=[bDT])
                      S.op(ACT, lambda h: h.activation(out=DT[:, :], in_=DT[:, :], func=AF.Ln, bias=1.0), reads=[bDT], writes=[bDT])
                      S.op(DVE, lambda h: h.tensor_tensor(out=DA[:, :].rearrange("p (a b) -> p a b", a=4), in0=DT[:, :].rearrange("p (a b) -> p a b", a=4),
                                                          in1=aneg[:, :].unsqueeze(1).to_broadcast([128, 4, 16]), op=ALU.mult),
                           reads=[bDT, bconst], writes=[bDT])

                      S.dma(ldg, GT[:, 0:8, :], gscr[:, tt * 512:(tt + 1) * 512].rearrange("(c p) t -> p c t", p=128),
                            reads=bgscr2, writes=[bGT])

                      if stop == 7:
                          raise _Stop()
                      def front(bb):
                          cs = slice(bb * 128, (bb + 1) * 128)
                          pxs = bank_bf(0)
                          for c in range(8):
                              S.op(PE, lambda h, c=c: h.transpose(out=pxs[:, c * 128:(c + 1) * 128], in_=XC[:, c, cs], identity=IDENT),
                                   reads=[bXC, bconst], writes=[bPS[0]])
                          S.op(ACT, lambda h: h.activation(out=XS_[bb % 2][:, :], in_=pxs, func=AF.Copy), reads=[bPS[0]], writes=[bXS_[bb % 2]])
                          S.op(DVE, lambda h: h.tensor_tensor(out=XDT_[bb % 2][:, :].rearrange("p (a b) -> p a b", a=16),
                                                              in0=pxs.rearrange("p (a b) -> p a b", a=16),
                                                              in1=DT[:, bb * 16:(bb + 1) * 16].unsqueeze(2).to_broadcast([128, 16, 64]), op=ALU.mult),
                               reads=[bPS[0], bDT], writes=[bXDT_[bb % 2]])
                          pbt = bank_bf(1)
                          for g in range(2):
                              S.op(PE, lambda h, g=g: h.transpose(out=pbt[:, g * 128:(g + 1) * 128], in_=XC[:, 8 + g, cs], identity=IDENT),
                                   reads=[bXC, bconst], writes=[bPS[1]])
                          S.op(ACT, lambda h: h.activation(out=BT_[bb % 2][:, :], in_=pbt[:, 0:256], func=AF.Copy), reads=[bPS[1]], writes=[bBT_[bb % 2]])
                          pa = bank(2)
                          S.op(PE, lambda h: h.matmul(pa[:, 0:16], lhsT=TRI_LE, rhs=DA[:, bb * 16:(bb + 1) * 16], start=True, stop=True),
                               reads=[bDT, bconst], writes=[bPS[2]])
                          S.op(PE, lambda h: h.matmul(pa[:, 16:32], lhsT=ONES, rhs=DA[:, bb * 16:(bb + 1) * 16], start=True, stop=True),
                               reads=[bDT, bconst], writes=[bPS[2]])
                          S.op(ACT, lambda h: h.activation(out=EA_[bb % 2][:, :], in_=pa[:, 0:32], func=AF.Exp), reads=[bPS[2]], writes=[bEA_[bb % 2]])
                          S.op(POOL, lambda h: h.tensor_tensor(
                              out=RR[:, :].rearrange("p (a b) -> p a b", a=16), in0=TRI_LE.unsqueeze(1).to_broadcast([128, 16, 128]),
                              in1=DA[:, bb * 16:(bb + 1) * 16].unsqueeze(2).to_broadcast([128, 16, 128]), op=ALU.mult),
                              reads=[bDT, bconst], writes=[bRR])
                          for q4 in range(4):
                              S.op(PE, lambda h, q4=q4: h.matmul(bank(4 + q4), lhsT=LT_STRICT,
                                                                 rhs=RR[:, q4 * 512:(q4 + 1) * 512],
                                                                 start=True, stop=True), reads=[bRR, bconst], writes=[bPS[4 + q4]])
                          S.op(ACT, lambda h: h.activation(out=ET[:, 0:1024], in_=bank2(4), func=AF.Exp),
                               reads=[bPS[4], bPS[5]], writes=[bET])
                          S.op(ACT, lambda h: h.activation(out=ET[:, 1024:2048], in_=bank2(6), func=AF.Exp),
                               reads=[bPS[6], bPS[7]], writes=[bET])
                          pc = bank(3)
                          for g in range(2):
                              S.op(PE, lambda h, g=g: h.matmul(pc[:, g * 128:(g + 1) * 128], lhsT=XC[:, 8 + g, cs], rhs=XC[:, 10 + g, cs],
                                                               start=True, stop=True), reads=[bXC], writes=[bPS[3]])
                          S.op(DVE, lambda h: h.tensor_tensor(out=CBM[:, :].rearrange("p (a b) -> p a b", a=2),
                                                              in0=pc[:, 0:256].rearrange("p (a b) -> p a b", a=2),
                                                              in1=TRI_LE.unsqueeze(1).to_broadcast([128, 2, 128]), op=ALU.mult),
                               reads=[bPS[3], bconst], writes=[bCBM])
                          for g in range(2):
                              S.op(DVE, lambda h, g=g: h.tensor_tensor(
                                  out=MT_[bb % 2][:, g * 1024:(g + 1) * 1024].rearrange("p (a b) -> p a b", a=8),
                                  in0=ET[:, g * 1024:(g + 1) * 1024].rearrange("p (a b) -> p a b", a=8),
                                  in1=CBM[:, g * 128:(g + 1) * 128].unsqueeze(1).to_broadcast([128, 8, 128]), op=ALU.mult),
                                  reads=[bET, bCBM], writes=[bMT_[bb % 2]])
                          S.op(DVE, lambda h: h.tensor_tensor(
                              out=XD_[bb % 2][:, :].rearrange("p (a b) -> p a b", a=16), in0=XDT_[bb % 2][:, :].rearrange("p (a b) -> p a b", a=16),
                              in1=ET[:, :].rearrange("p (a b) -> p a b", a=16)[:, :, 127:128].to_broadcast([128, 16, 64]), op=ALU.mult),
                              reads=[bXDT_[bb % 2], bET], writes=[bXD_[bb % 2]])

                      def back(bb):
                          cs = slice(bb * 128, (bb + 1) * 128)
                          py = bank2(2)
                          S.op(POOL, lambda h: h.tensor_tensor(out=XSD[:, :].rearrange("p (a b) -> p a b", a=16),
                                                               in0=XS_[bb % 2][:, :].rearrange("p (a b) -> p a b", a=16),
                                                               in1=DSK.unsqueeze(2).to_broadcast([128, 16, 64]), op=ALU.mult),
                               reads=[bXS_[bb % 2], bconst], writes=[bXSD])
                          for g in range(2):
                              S.op(PE, lambda h, g=g: h.matmul(py[:, g * 512:(g + 1) * 512], lhsT=IDENT, rhs=XSD[:, g * 512:(g + 1) * 512],
                                                               start=True, stop=False, skip_group_check=True),
                                   reads=[bXSD, bconst], writes=[bPS[2 + g]])
                          for hd in range(16):
                              S.op(PE, lambda h, hd=hd: h.matmul(py[:, hd * 64:(hd + 1) * 64], lhsT=MT_[bb % 2][:, hd * 128:(hd + 1) * 128],
                                                                 rhs=XDT_[bb % 2][:, hd * 64:(hd + 1) * 64], start=False, stop=True,
                                                                 skip_group_check=True),
                                   reads=[bMT_[bb % 2], bXDT_[bb % 2]], writes=[bPS[2], bPS[3]])
                          pyo = bank2(4)
                          for g in range(2):
                              S.op(PE, lambda h, g=g: h.matmul(pyo[:, g * 512:(g + 1) * 512], lhsT=XC[:, 10 + g, cs],
                                                               rhs=STB[:, g * 512:(g + 1) * 512], start=True, stop=True),
                                   reads=[bXC, bSTB], writes=[bPS[4 + g]])
                          S.op(DVE, lambda h: h.tensor_tensor(out=T1[:, :].rearrange("p (a b) -> p a b", a=16),
                                                              in0=pyo.rearrange("p (a b) -> p a b", a=16),
                                                              in1=EA_[bb % 2][:, 0:16].unsqueeze(2).to_broadcast([128, 16, 64]), op=ALU.mult),
                               reads=[bPS[4], bPS[5], bEA_[bb % 2]], writes=[bT1])
                          S.op(DVE, lambda h: h.tensor_tensor(out=T1[:, :], in0=py, in1=T1[:, :], op=ALU.add),
                               reads=[bPS[2], bPS[3], bT1], writes=[bT1])
                          S.op(DVE, lambda h: h.tensor_tensor(out=T1[:, :], in0=T1[:, :], in1=SZ[:, bb, :], op=ALU.mult),
                               reads=[bT1, bSZ], writes=[bT1])
                          S.op(ACT, lambda h: h.activation(out=T2[:, :], in_=T1[:, :], func=AF.Square, accum_out=SSQ2[:, bb:bb + 1]),
                               reads=[bT1, bT2], writes=[bT2, bSSQ2])
                          S.op(ACT, lambda h: h.activation(out=GB[:, :], in_=T1[:, :], func=AF.Copy), reads=[bT1], writes=[bGB])
                          pg = bank_bf(0)
                          for c in range(8):
                              S.op(PE, lambda h, c=c: h.transpose(out=pg[:, c * 128:(c + 1) * 128], in_=GB[:, c * 128:(c + 1) * 128],
                                                                  identity=IDENT), reads=[bGB, bconst], writes=[bPS[0]])
                          S.op(DVE, lambda h: h.tensor_tensor(out=GT[:, 8:16, cs], in0=pg.rearrange("p (a b) -> p a b", a=8),
                                                              in1=SSDW.unsqueeze(2).to_broadcast([128, 8, 128]), op=ALU.mult),
                               reads=[bPS[0], bconst], writes=[bGT])
                          pst = bank2(6)
                          for g in range(2):
                              S.op(PE, lambda h, g=g: h.matmul(pst[:, g * 512:(g + 1) * 512], lhsT=BT_[bb % 2][:, g * 128:(g + 1) * 128],
                                                               rhs=XD_[bb % 2][:, g * 512:(g + 1) * 512], start=True, stop=True),
                                   reads=[bBT_[bb % 2], bXD_[bb % 2]], writes=[bPS[6 + g]])
                          S.op(DVE, lambda h: h.tensor_tensor(out=ST[:, :].rearrange("p (a b) -> p a b", a=16),
                                                              in0=ST[:, :].rearrange("p (a b) -> p a b", a=16),
                                                              in1=EA_[bb % 2][:, 16:32].unsqueeze(2).to_broadcast([128, 16, 64]), op=ALU.mult),
                               reads=[bST, bEA_[bb % 2]], writes=[bST])
                          S.op(DVE, lambda h: h.tensor_tensor(out=ST[:, :], in0=pst, in1=ST[:, :], op=ALU.add),
                               reads=[bPS[6], bPS[7], bST], writes=[bST])
                          S.op(ACT, lambda h: h.activation(out=STB[:, :], in_=ST[:, :], func=AF.Copy), reads=[bST], writes=[bSTB])


                      front(0)
                      for bb in range(4):
                          if bb + 1 < 4:
                              front(bb + 1)
                          back(bb)
                      if stop == 14:
                          raise _Stop()
                      S.op(ACT, lambda h: h.activation(out=SSQ2[:, 4:8], in_=SSQ2[:, 0:4], func=AF.Ln, scale=1.0 / 1024, bias=EPS),
                           reads=[bSSQ2], writes=[bSSQ2])
                      S.op(ACT, lambda h: h.activation(out=SSQ2[:, 4:8], in_=SSQ2[:, 4:8], func=AF.Exp, scale=-0.5),
                           reads=[bSSQ2], writes=[bSSQ2])
                      if stop == 15:
                          raise _Stop()
                      for bb in range(4):
                          blk = tt * 4 + bb
                          cs = slice(bb * 128, (bb + 1) * 128)
                          s = xcnt[0] % 2
                          xcnt[0] += 1
                          S.dma(ldx[s], xt[s][:, :], x_d[seq, blk * 128:(blk + 1) * 128, :], writes=[bxt[s]])
                          for n in range(2):
                              p1 = bank(n * 2); p2 = bank(n * 2 + 1)
                              for c in range(8):
                                  S.op(PE, lambda h, c=c: h.matmul(p1, lhsT=GT[:, c, cs], rhs=wout[:, c, n * 512:(n + 1) * 512],
                                                                   start=(c == 0), stop=(c == 7)), reads=[bGT, bwout], writes=[bPS[n * 2]])
                              for c in range(8, 16):
                                  S.op(PE, lambda h, c=c: h.matmul(p2, lhsT=GT[:, c, cs], rhs=wout[:, c, n * 512:(n + 1) * 512],
                                                                   start=(c == 8), stop=(c == 15)), reads=[bGT, bwout],
                                       writes=[bPS[n * 2 + 1]])
                              ns = slice(n * 512, (n + 1) * 512)
                              S.op(DVE, lambda h: h.scalar_tensor_tensor(out=xt[s][:, ns], in0=p1, scalar=RSB[:, blk:blk + 1],
                                                                         in1=xt[s][:, ns], op0=ALU.mult, op1=ALU.add),
                                   reads=[bPS[n * 2], bRSB, bxt[s]], writes=[bxt[s]])
                              S.op(DVE, lambda h: h.scalar_tensor_tensor(out=xt[s][:, ns], in0=p2, scalar=SSQ2[:, 4 + bb:5 + bb],
                                                                         in1=xt[s][:, ns], op0=ALU.mult, op1=ALU.add),
                                   reads=[bPS[n * 2 + 1], bSSQ2, bxt[s]], writes=[bxt[s]])
                          S.dma(sto[s], out_d[seq, blk * 128:(blk + 1) * 128, :], xt[s][:, :], reads=[bxt[s]])
                  S.barrier()
      except _Stop:
        pass
      S.barrier()
      build.stats = (S.ninst, S.nwaits, S.nsem)
    return nc


_CACHE = {}


def _run(x, w_in, w_out, prm, L, nseq, ncores):
    key = (L, nseq)
    if key not in _CACHE:
        _CACHE[key] = build(L, nseq)
    nc = _CACHE[key]
    cf, cb = _consts()
    in_maps = []
    for c in range(ncores):
        in_maps.append({"x": np.ascontiguousarray(x[c * nseq:(c + 1) * nseq]), "w_in": w_in, "w_out": w_out,
                        "prm": prm, "cf": cf, "cb": cb})
    res = run_bass_kernel_spmd(nc, in_maps, core_ids=list(range(ncores)))
    return np.concatenate([r["out"] for r in res.results], axis=0)


def kernel(x, norm_w, w_in, q_norm_w, k_norm_w, conv_w, conv_b, dt_bias, A_log, D_skip, sb_norm_w, ssd_norm_w, w_out):
    f = lambda a: np.ascontiguousarray(np.asarray(a, dtype=np.float32))
    x = f(x)
    prm = _params(f(norm_w)[0], f(q_norm_w)[0], f(k_norm_w)[0], f(conv_w)[0], f(conv_b)[0], f(dt_bias)[0],
                  f(A_log)[0], f(D_skip)[0], f(sb_norm_w)[0], f(ssd_norm_w)[0])
    B, L, _ = x.shape
    nseq = B // NCORES
    out = _run(x, f(w_in)[0], f(w_out)[0], prm, L, nseq, NCORES)
    return out.astype(np.float32)
```
